# Optimizing a Trainium2 kernel written in Bass

```python
import jax, jax.numpy as jnp
from jax import lax
import numpy as np

D_MODEL = 1024
BATCH = 4
SEQ = 4096
DEPTH = 2

N_MIXERS = 2
CHUNK = 64
RMS_EPS = 1e-6

RWKV_HEAD = 64
RWKV_HEADS = D_MODEL // RWKV_HEAD
DECAY_LORA = max(32, int(round(1.8 * D_MODEL ** 0.5 / 32)) * 32)
A_LORA = max(32, int(round(1.8 * D_MODEL ** 0.5 / 32)) * 32)
GATE_LORA = max(32, int(round(0.6 * D_MODEL ** 0.8 / 32)) * 32)
GN_EPS = 64e-5
N_SHIFT_MIX = 6

SGU_BLOCK = 128
D_SGU = D_MODEL
SGU_GROUPS = 8
SGU_GROUP_DIM = D_SGU // SGU_GROUPS
LN_EPS = 1e-5

D_FF = -(-8 * D_MODEL // (3 * 256)) * 256

N_A = (DEPTH + 1) // N_MIXERS + (0 if N_MIXERS == 2 else 0)
N_B = DEPTH // N_MIXERS

kernel_name = "rwkv7_sgu_interleaved_sandwich_trunk"


def _rmsnorm(x, g):
    x32 = x.astype(jnp.float32)
    y = x32 * lax.rsqrt(jnp.mean(x32 * x32, axis=-1, keepdims=True) + RMS_EPS)
    return (y * g.astype(jnp.float32)).astype(x.dtype)


def _token_shift(x):
    return jnp.pad(x[:, :-1], ((0, 0), (1, 0), (0, 0)))


def _rwkv7_scan(r, w, k, v, a, b):
    B, T, H, N = r.shape

    def step(S, inp):
        r_t, w_t, k_t, v_t, a_t, b_t = inp
        sa = jnp.einsum('bhij,bhj->bhi', S, a_t)
        S = (S * w_t[:, :, None, :]
             + sa[..., None] * b_t[:, :, None, :]
             + v_t[..., None] * k_t[:, :, None, :])
        y = jnp.einsum('bhij,bhj->bhi', S, r_t)
        return S, y

    xs = tuple(jnp.moveaxis(t, 1, 0) for t in (r, w, k, v, a, b))
    S0 = jnp.zeros((B, H, N, N), jnp.float32)
    _, y = lax.scan(step, S0, xs)
    return jnp.moveaxis(y, 0, 1)


def _rwkv7_time_mix(x, mix, w_rkv, w0, w1, w2, a0, a1, a2, g1, g2,
                    k_k, k_a, r_k, gn_w, gn_b, w_o):
    B, T, D = x.shape
    H, N = RWKV_HEADS, RWKV_HEAD
    f32 = jnp.float32
    xx = _token_shift(x) - x
    xs = x[None] + xx[None] * mix[:, None, None, :]
    r, k, v = jnp.einsum('cbtd,cde->cbte', xs[:3], w_rkv)
    xw, xa, xg = xs[3], xs[4], xs[5]
    w_raw = (w0 + jnp.tanh(xw @ w1) @ w2).astype(f32)
    w_log = -jax.nn.softplus(-w_raw) - 0.5
    decay = jnp.exp(-jnp.exp(w_log))
    a = jax.nn.sigmoid(a0 + (xa @ a1) @ a2)
    g = jax.nn.sigmoid(xg @ g1) @ g2
    heads = lambda t: t.reshape(B, T, H, N).astype(f32)
    kk = heads(k * k_k)
    kk = kk / jnp.maximum(jnp.sqrt(jnp.sum(kk * kk, axis=-1, keepdims=True)), 1e-12)
    k = k * (1.0 + (a - 1.0) * k_a)
    r_h, k_h, v_h, a_h = heads(r), heads(k), heads(v), heads(a)
    y = _rwkv7_scan(r_h, decay.reshape(B, T, H, N), k_h, v_h, -kk, kk * a_h)
    mu = jnp.mean(y, axis=-1, keepdims=True)
    var = jnp.mean(jnp.square(y - mu), axis=-1, keepdims=True)
    y = (y - mu) * lax.rsqrt(var + GN_EPS)
    y = y * gn_w.reshape(H, N).astype(f32) + gn_b.reshape(H, N).astype(f32)
    y = y + jnp.sum(r_h * k_h * r_k.astype(f32), axis=-1, keepdims=True) * v_h
    y = y.reshape(B, T, D).astype(x.dtype)
    return (y * g) @ w_o


def _sgu_mixer(x, w_in, b_in, ln_w, ln_b, ws, bs, w_out):
    B, T, D = x.shape
    h = jax.nn.gelu(x @ w_in + b_in, approximate=False)
    u, v = jnp.split(h, 2, axis=-1)
    v32 = v.astype(jnp.float32)
    mu = jnp.mean(v32, axis=-1, keepdims=True)
    var = jnp.mean(jnp.square(v32 - mu), axis=-1, keepdims=True)
    v = ((v32 - mu) * lax.rsqrt(var + LN_EPS) * ln_w.astype(jnp.float32)
         + ln_b.astype(jnp.float32)).astype(x.dtype)
    n_blk = T // SGU_BLOCK
    vb = v.reshape(B, n_blk, SGU_BLOCK, SGU_GROUPS, SGU_GROUP_DIM)
    chunk_id = jnp.arange(SGU_BLOCK) // CHUNK
    mask = chunk_id[None, :] <= chunk_id[:, None]
    ws_m = jnp.where(mask[None], ws, jnp.zeros((), ws.dtype))
    s = jnp.einsum('gij,bnjgc->bnigc', ws_m, vb) + bs.T[None, None, :, :, None]
    s = s.reshape(B, T, D_SGU)
    return (u * s) @ w_out


def _swiglu(x, w_gate, w_up, w_down):
    return (jax.nn.silu(x @ w_gate) * (x @ w_up)) @ w_down


def setup_inputs(seed: int = 0) -> dict:
    key = jax.random.key(seed)
    ks = jax.random.split(key, 32)
    D = D_MODEL
    nrm = lambda k, shape, scale: jax.random.normal(k, shape, jnp.float32) * scale
    return {
        "x": nrm(ks[0], (BATCH, SEQ, D), 1.0),
        "norm_gains": 1.0 + nrm(ks[1], (DEPTH, 4, D), 0.1),
        "rwkv_mix": jax.random.uniform(ks[2], (N_A, N_SHIFT_MIX, D), jnp.float32),
        "rwkv_w_rkv": nrm(ks[3], (N_A, 3, D, D), D ** -0.5),
        "rwkv_w0": jax.random.uniform(ks[4], (N_A, D), jnp.float32, -6.5, -1.5),
        "rwkv_w1": nrm(ks[5], (N_A, D, DECAY_LORA), D ** -0.5),
        "rwkv_w2": nrm(ks[6], (N_A, DECAY_LORA, D), 0.5 * DECAY_LORA ** -0.5),
        "rwkv_a0": nrm(ks[7], (N_A, D), 0.1),
        "rwkv_a1": nrm(ks[8], (N_A, D, A_LORA), D ** -0.5),
        "rwkv_a2": nrm(ks[9], (N_A, A_LORA, D), 0.5 * A_LORA ** -0.5),
        "rwkv_g1": nrm(ks[10], (N_A, D, GATE_LORA), D ** -0.5),
        "rwkv_g2": nrm(ks[11], (N_A, GATE_LORA, D), GATE_LORA ** -0.5),
        "rwkv_k_k": 0.85 + nrm(ks[12], (N_A, D), 0.05),
        "rwkv_k_a": 1.0 + nrm(ks[13], (N_A, D), 0.05),
        "rwkv_r_k": -0.04 + nrm(ks[14], (N_A, RWKV_HEADS, RWKV_HEAD), 0.02),
        "rwkv_gn_w": 1.0 + nrm(ks[15], (N_A, D), 0.1),
        "rwkv_gn_b": nrm(ks[16], (N_A, D), 0.02),
        "rwkv_w_o": nrm(ks[17], (N_A, D, D), D ** -0.5),
        "sgu_w_in": nrm(ks[18], (N_B, D, 2 * D_SGU), D ** -0.5),
        "sgu_b_in": nrm(ks[19], (N_B, 2 * D_SGU), 0.02),
        "sgu_ln_w": 1.0 + nrm(ks[20], (N_B, D_SGU), 0.1),
        "sgu_ln_b": nrm(ks[21], (N_B, D_SGU), 0.02),
        "sgu_ws": nrm(ks[22], (N_B, SGU_GROUPS, SGU_BLOCK, SGU_BLOCK), SGU_BLOCK ** -0.5),
        "sgu_bs": 1.0 + nrm(ks[23], (N_B, SGU_GROUPS, SGU_BLOCK), 0.1),
        "sgu_w_out": nrm(ks[24], (N_B, D_SGU, D), D_SGU ** -0.5),
        "ffn_w_gate": nrm(ks[25], (DEPTH, D, D_FF), D ** -0.5),
        "ffn_w_up": nrm(ks[26], (DEPTH, D, D_FF), D ** -0.5),
        "ffn_w_down": nrm(ks[27], (DEPTH, D_FF, D), D_FF ** -0.5),
    }


def reference(x, norm_gains,
              rwkv_mix, rwkv_w_rkv, rwkv_w0, rwkv_w1, rwkv_w2, rwkv_a0, rwkv_a1, rwkv_a2,
              rwkv_g1, rwkv_g2, rwkv_k_k, rwkv_k_a, rwkv_r_k, rwkv_gn_w, rwkv_gn_b, rwkv_w_o,
              sgu_w_in, sgu_b_in, sgu_ln_w, sgu_ln_b, sgu_ws, sgu_bs, sgu_w_out,
              ffn_w_gate, ffn_w_up, ffn_w_down):
    for i in range(DEPTH):
        j = i // N_MIXERS
        h = _rmsnorm(x, norm_gains[i, 0])
        if i % N_MIXERS == 0:
            h = _rwkv7_time_mix(h, rwkv_mix[j], rwkv_w_rkv[j], rwkv_w0[j], rwkv_w1[j],
                                rwkv_w2[j], rwkv_a0[j], rwkv_a1[j], rwkv_a2[j], rwkv_g1[j],
                                rwkv_g2[j], rwkv_k_k[j], rwkv_k_a[j], rwkv_r_k[j],
                                rwkv_gn_w[j], rwkv_gn_b[j], rwkv_w_o[j])
        else:
            h = _sgu_mixer(h, sgu_w_in[j], sgu_b_in[j], sgu_ln_w[j], sgu_ln_b[j],
                           sgu_ws[j], sgu_bs[j], sgu_w_out[j])
        x = x + _rmsnorm(h, norm_gains[i, 1])
        h = _rmsnorm(x, norm_gains[i, 2])
        h = _swiglu(h, ffn_w_gate[i], ffn_w_up[i], ffn_w_down[i])
        x = x + _rmsnorm(h, norm_gains[i, 3])
    return x
```

```python
import numpy as np
from contextlib import ExitStack
import concourse.bass as bass
import concourse.mybir as mybir
from concourse.bass_utils import run_bass_kernel_spmd

F32 = mybir.dt.float32
BF16 = mybir.dt.bfloat16
AF = mybir.ActivationFunctionType
ALU = mybir.AluOpType

D = 1024
NTOK = 2048
DFF = 2816
NF = DFF // 128
CDEC = float(np.exp(-0.5))
DEBUG = {"stage": 99, "blocks": list(range(16))}


class Rec:
    def __init__(self):
        self.ops = []

    def add(self, eng, fn, r=(), w=(), dma=False):
        self.ops.append(dict(eng=eng, fn=fn, r=tuple(r), w=tuple(w), dma=dma))


class Sched:
    ENGS = ["pe", "dve", "act", "pool", "sp"]
    NDS = 24

    def __init__(self, nc, st):
        self.nc = nc
        self.esem = {e: st.enter_context(nc.semaphore("s_" + e)) for e in self.ENGS}
        self.dsem = [st.enter_context(nc.semaphore("d_%d" % k)) for k in range(self.NDS)]
        self.ecnt = {e: 0 for e in self.ENGS}
        self.dcnt = [0] * self.NDS
        self.nd = {"sp": 0, "pool": 0}

    def flush(self, rec, final=False):
        ops = rec.ops
        rec.ops = []
        n = len(ops)
        engs, NDS = self.ENGS, self.NDS
        pos = [0] * n
        cnt = {e: 0 for e in engs}
        last_of = {}
        for i, o in enumerate(ops):
            pos[i] = cnt[o["eng"]]
            cnt[o["eng"]] += 1
            last_of[o["eng"]] = i
        base_e = dict(self.ecnt)
        base_d = list(self.dcnt)
        last_w, readers = {}, {}
        ps_readers = {}
        deps = [dict() for _ in range(n)]
        dma_prev = [None] * NDS
        dma_sem_of, dma_val_of = {}, {}
        for i, o in enumerate(ops):
            dd = deps[i]
            for k in o["r"]:
                j = last_w.get(k)
                if j is not None:
                    dd[j] = True
            for k in o["w"]:
                j = last_w.get(k)
                if j is not None and j not in dd:
                    dd[j] = False
                for j in readers.get(k, ()):
                    if j not in dd:
                        dd[j] = False
            for k in o["r"]:
                if isinstance(k, tuple) and k[0] == "ps":
                    for e2, j in ps_readers.get(k, {}).items():
                        if e2 != o["eng"] and j not in dd:
                            dd[j] = False
            dd.pop(i, None)
            for k in o["r"]:
                readers.setdefault(k, []).append(i)
                if isinstance(k, tuple) and k[0] == "ps":
                    ps_readers.setdefault(k, {})[o["eng"]] = i
            for k in o["w"]:
                last_w[k] = i
                readers[k] = []
                if isinstance(k, tuple) and k[0] == "ps":
                    ps_readers[k] = {}
            if o["dma"]:
                half = NDS // 2
                s = (self.nd[o["eng"]] % half) + (0 if o["eng"] == "sp" else half)
                self.nd[o["eng"]] += 1
                if dma_prev[s] is not None:
                    dd[dma_prev[s]] = True
                dma_prev[s] = i
                self.dcnt[s] += 1
                dma_sem_of[i] = s
                dma_val_of[i] = 16 * self.dcnt[s]
        need_inc = [False] * n
        for i, o in enumerate(ops):
            keep = {}
            for j, raw in deps[i].items():
                p = ops[j]
                if p["dma"]:
                    keep[j] = raw
                    continue
                if p["eng"] == o["eng"]:
                    if o["dma"]:
                        keep[j] = raw
                    elif o["eng"] == "pe":
                        continue
                    elif raw or not DEBUG.get("relax", False):
                        keep[j] = raw
                    continue
                keep[j] = raw
            deps[i] = keep
            for j in keep:
                if not ops[j]["dma"]:
                    need_inc[j] = True
        for e, i in last_of.items():
            if not ops[i]["dma"]:
                need_inc[i] = True
        ordinal = [0] * n
        for i, o in enumerate(ops):
            if need_inc[i]:
                self.ecnt[o["eng"]] += 1
                ordinal[i] = self.ecnt[o["eng"]]
        waited = {e: {} for e in engs}
        waits = [None] * n
        for i, o in enumerate(ops):
            wl = {}
            for j in deps[i]:
                p = ops[j]
                if p["dma"]:
                    key, val = ("d", dma_sem_of[j]), dma_val_of[j]
                else:
                    key, val = ("e", p["eng"]), ordinal[j]
                if wl.get(key, 0) < val:
                    wl[key] = val
            out = []
            wd = waited[o["eng"]]
            for key, val in wl.items():
                if wd.get(key, 0) >= val:
                    continue
                wd[key] = val
                out.append((key, val))
            waits[i] = out
        esem, dsem = self.esem, self.dsem
        end_d = list(self.dcnt)

        def run_engine(ename, eobj):
            for e2 in engs:
                if e2 != ename and base_e[e2] > 0:
                    eobj.wait_ge(esem[e2], base_e[e2])
            for s in range(NDS):
                if base_d[s] > 0:
                    eobj.wait_ge(dsem[s], 16 * base_d[s])
            for i, o in enumerate(ops):
                if o["eng"] != ename:
                    continue
                for key, val in waits[i]:
                    sem = dsem[key[1]] if key[0] == "d" else esem[key[1]]
                    eobj.wait_ge(sem, val)
                ins = o["fn"](eobj)
                if o["dma"]:
                    ins.then_inc(dsem[dma_sem_of[i]], 16)
                elif need_inc[i]:
                    ins.then_inc(esem[ename], 1)
            if final and ename == "sp":
                for s in range(NDS):
                    if end_d[s] > 0:
                        eobj.wait_ge(dsem[s], 16 * end_d[s])

        with self.nc.Block() as block:
            @block.sync
            def _(e):
                run_engine("sp", e)

            @block.gpsimd
            def _(e):
                run_engine("pool", e)

            @block.tensor
            def _(e):
                run_engine("pe", e)

            @block.vector
            def _(e):
                run_engine("dve", e)

            @block.scalar
            def _(e):
                run_engine("act", e)


PC = dict(mix=0, g00=48, w0=56, a0=64, kk=72, ka=80, rk=88, gnw=96, gnb=104,
          g02=112, g10=120, g12=128, binu=136, lnw=144, lnb=152, hm=160)
NPC = 164


def build_program():
    nc = bass.Bass("TRN2", target_bir_lowering=False)
    st = ExitStack()
    rec = Rec()
    sched = Sched(nc, st)

    def din(name, shape):
        return nc.dram_tensor(name, list(shape), F32, kind="ExternalInput").ap()

    xo = din("xo", [NTOK, D])
    xp = din("xp", [NTOK, D])
    pcol_d = din("pcol", [128, NPC])
    brow_d = din("brow", [4, 128, D])
    cst_d = din("cst", [128, 128 + 3 * 512 + 256 + 128 + 128 + 64])
    rowv_d = din("rowv", [1, 2048 + 128])
    w_rkv_d = din("w_rkv", [3, D, D])
    w_o_d = din("w_o", [D, D])
    w1_d = din("w1", [D, 64])
    w2_d = din("w2", [64, D])
    a1_d = din("a1", [D, 64])
    a2_d = din("a2", [64, D])
    g1_d = din("g1", [D, 160])
    g2_d = din("g2", [160, D])
    sgu_in_d = din("sgu_in", [D, 2 * D])
    sgu_ws_d = din("sgu_ws", [8, 128, 128])
    sgu_out_d = din("sgu_out", [D, D])
    ffg_d = din("ffg", [2, D, DFF])
    ffu_d = din("ffu", [2, D, DFF])
    ffd_d = din("ffd", [2, DFF, D])
    out_d = nc.dram_tensor("out", [NTOK, D], F32, kind="ExternalOutput").ap()
    x1s = nc.dram_tensor("x1s", [NTOK, D], F32, kind="Internal").ap()

    def sb(name, shape, dt=F32):
        return st.enter_context(nc.sbuf_tensor("sb_" + name, list(shape), dt))

    ps = [st.enter_context(nc.psum_tensor("ps%d" % i, [128, 512], F32)) for i in range(8)]

    def mm(out, lhsT, rhs, start, stop, r, w):
        rec.add("pe", lambda e: e.matmul(out, lhsT, rhs, start=start, stop=stop), r, w)

    def act(out, in_, func, r, w, bias=None, scale=None, accum=None):
        kw = {}
        if bias is not None:
            kw["bias"] = bias
        if scale is not None:
            kw["scale"] = scale
        if accum is not None:
            kw["accum_out"] = accum
        rec.add("act", lambda e: e.activation(out, in_, func, **kw), r, w)

    def tt(eng, out, in0, in1, op, r, w):
        rec.add(eng, lambda e: e.tensor_tensor(out, in0, in1, op), r, w)

    def ts(eng, out, in0, s1, s2, op0, op1, r, w):
        if s2 is None:
            rec.add(eng, lambda e: e.tensor_scalar(out, in0, s1, None, op0), r, w)
        else:
            rec.add(eng, lambda e: e.tensor_scalar(out, in0, s1, s2, op0, op1), r, w)

    def stt(out, in0, scalar, in1, op0, op1, r, w):
        rec.add("dve", lambda e: e.scalar_tensor_tensor(out, in0, scalar, in1, op0, op1), r, w)

    def cp(eng, out, in_, r, w):
        rec.add(eng, lambda e: e.tensor_copy(out, in_), r, w)

    def dma(eng, out, in_, r, w):
        rec.add(eng, lambda e: e.dma_start(out=out, in_=in_), r, w, dma=True)

    pcol = sb("pcol", [128, NPC])
    cst = sb("cstf", [128, 256 + 128 + 64])
    identb = sb("identb", [128, 128], BF16)
    maskb = sb("maskb", [128, 3, 512], BF16)
    bonesb = sb("bonesb", [128, 128], BF16)
    selb = sb("selb", [128, 128], BF16)
    ifoldb = sb("ifoldb", [128, 64], BF16)
    dma("sp", pcol[:, :], pcol_d[:, :], [], ["pcol"])
    o_id, o_m, o_keep, o_bo, o_sel, o_if = 0, 128, 128 + 1536, 128 + 1536 + 256, 128 + 1536 + 384, 128 + 1536 + 512
    dma("sp", cst[:, 0:256], cst_d[:, o_keep:o_keep + 256], [], ["cst"])
    dma("sp", cst[:, 256:448], cst_d[:, o_sel:o_sel + 192], [], ["cst"])
    dma("pool", identb[:, :], cst_d[:, o_id:o_id + 128], [], ["identb"])
    dma("pool", maskb[:, :, :], cst_d[:, o_m:o_m + 1536].rearrange("p (a b) -> p a b", a=3), [], ["maskb"])
    dma("pool", bonesb[:, :], cst_d[:, o_bo:o_bo + 128], [], ["bonesb"])
    dma("pool", selb[:, :], cst_d[:, o_sel:o_sel + 128], [], ["selb"])
    dma("pool", ifoldb[:, :], cst_d[:, o_if:o_if + 64], [], ["ifoldb"])
    keepm = cst[:, 0:256]
    self32 = cst[:, 256:384]
    ifold = cst[:, 384:448]
    mSL, mLE, mGT = maskb[:, 0, :], maskb[:, 1, :], maskb[:, 2, :]

    def pc(name, k):
        c = PC[name] + k
        return pcol[:, c:c + 1]

    evac_flip = [0]

    def evac_copy(out, in_, r, w):
        evac_flip[0] ^= 1
        if evac_flip[0]:
            act(out, in_, AF.Copy, r, w)
        else:
            cp("dve", out, in_, r, w)

    epsT = sb("epsT", [128, 5])
    for ci, v in enumerate([1e-6, 1e-24, 64e-5, 1e-5, 1.0]):
        rec.add("pool", lambda e, ci=ci, v=v: e.memset(epsT[:, ci:ci + 1], v), [], ["epsT"])

    negc = sb("negc", [128, 16])
    rec.add("dve", lambda e: e.tensor_scalar(negc[:, 0:8], pcol[:, PC["w0"]:PC["w0"] + 8], -1.0, None, ALU.mult), ["pcol"], ["negc"])
    rec.add("dve", lambda e: e.tensor_scalar(negc[:, 8:16], pcol[:, PC["a0"]:PC["a0"] + 8], -1.0, None, ALU.mult), ["pcol", "negc"], ["negc"])

    def sigmoid_el(out, in_, negbias, r, w):
        act(out, in_, AF.Exp, r + ["negc"], w, scale=-1.0, bias=negbias)
        act(out, out, AF.Ln, w + ["epsT"], w, bias=epsT[:, 4:5])
        act(out, out, AF.Exp, w, w, scale=-1.0)

    def rsqrt(out, in_, scale, epscol, r, w, tag):
        act(out, in_, AF.Ln, r + ["epsT"], w, scale=scale, bias=epsT[:, epscol:epscol + 1])
        act(out, out, AF.Exp, w, w, scale=-0.5)

    junk = sb("junk", [128, 1024], BF16)

    def rms_stats(src_aps, ssq, rstd, ncols, rkeys, tag):
        for n, a in enumerate(src_aps):
            act(junk[:, :], a, AF.Square, rkeys[n] + ["cst"], ["junk", (tag, "ssq", n)], accum=ssq[:, n:n + 1])
        rsqrt(rstd[:, 0:ncols], ssq[:, 0:ncols], 1.0 / D, 0, [(tag, "ssq", n) for n in range(ncols)], [(tag, "rstd")], tag)

    stage = DEBUG["stage"]

    if DEBUG.get("skip_rwkv"):
        with ExitStack() as sa0:
            tmpx = sa0.enter_context(nc.sbuf_tensor("sa0_tmpx", [128, 4, D], F32))
            for g in range(4):
                dma("sp", tmpx[:, :, :], xo.rearrange("(n p) d -> p n d", p=128)[:, g * 4:(g + 1) * 4, :], [], ["tmpx"])
                dma("sp", x1s.rearrange("(n p) d -> p n d", p=128)[:, g * 4:(g + 1) * 4, :], tmpx[:, :, :], ["tmpx"], [("x1s", g)])
            sched.flush(rec, final=False)
    with ExitStack() as sa:
        if DEBUG.get("skip_rwkv"):
            DEBUG["blocks"] = []
        def sba(name, shape, dt=F32):
            return sa.enter_context(nc.sbuf_tensor("sa_" + name, list(shape), dt))

        Wr = sba("Wrkv", [128, 3, 8, D], BF16)
        Wo = sba("Wo", [128, 8, D], BF16)
        w1s = sba("w1s", [128, 8, 64], BF16)
        a1s = sba("a1s", [128, 8, 64], BF16)
        g1s = sba("g1s", [128, 8, 160], BF16)
        w2s = sba("w2s", [64, D], BF16)
        a2s = sba("a2s", [64, D], BF16)
        g2a = sba("g2a", [128, D], BF16)
        g2b = sba("g2b", [32, D], BF16)
        for c in range(3):
            for hf in range(2):
                dma("pool", Wr[:, c, hf * 4:(hf + 1) * 4, :],
                    w_rkv_d[c].rearrange("(k p) e -> p k e", p=128)[:, hf * 4:(hf + 1) * 4, :], [], [("Wr", c, hf)])
        dma("pool", w1s[:, :, :], w1_d.rearrange("(k p) e -> p k e", p=128), [], ["w1s"])
        dma("pool", a1s[:, :, :], a1_d.rearrange("(k p) e -> p k e", p=128), [], ["a1s"])
        dma("pool", g1s[:, :, :], g1_d.rearrange("(k p) e -> p k e", p=128), [], ["g1s"])
        dma("pool", w2s[:, :], w2_d[:, :], [], ["w2s"])
        dma("pool", a2s[:, :], a2_d[:, :], [], ["a2s"])
        dma("pool", g2a[:, :], g2_d[0:128, :], [], ["g2a"])
        dma("pool", g2b[:, :], g2_d[128:160, :], [], ["g2b"])
        for hf in range(2):
            dma("pool", Wo[:, hf * 4:(hf + 1) * 4, :],
                w_o_d.rearrange("(k p) e -> p k e", p=128)[:, hf * 4:(hf + 1) * 4, :], [], [("Wo", hf)])
        Wrk = lambda c: [("Wr", c, 0), ("Wr", c, 1)]
        Wok = [("Wo", 0), ("Wo", 1)]

        growb = sba("growb", [128, D])
        dma("sp", growb[:, :], brow_d[0], [], ["growb"])
        xin = sba("xin", [128, 2, D])
        xs_tm = sba("xs_tm", [128, 2, D], BF16)
        xnT = sba("xnT", [128, 8, 257], BF16)
        xx = sba("xx", [128, 8, 256], BF16)
        xsR = sba("xsR", [128, 8, 256], BF16)
        xsK = sba("xsK", [128, 8, 256], BF16)
        xsT = sba("xsT", [128, 8, 256], BF16)
        v_tm = sba("v_tm", [128, 2, D], BF16)
        v_fmp = [sba("v_fm%d" % p, [128, 256], BF16) for p in range(2)]
        hid_w = sba("hid_w", [64, 256], BF16)
        hid_a = sba("hid_a", [64, 256], BF16)
        hid_g = sba("hid_g", [128, 256], BF16)
        hid_g2 = sba("hid_g2", [32, 256], BF16)
        y_tmp = [sba("y_tm%d" % p, [128, 256]) for p in range(2)]
        ysqp = [sba("ysq%d" % p, [128, 256]) for p in range(2)]
        ssq = sba("ssq", [128, 4])
        rstd = sba("rstd", [128, 4])
        NT_ = 12
        T = [sba("T%d" % i, [128, 256]) for i in range(NT_)]
        csC = sba("csC", [128, 2])
        WC = sba("WC", [128, 2])
        DWp = [sba("DW%d" % p, [128, 2, 64]) for p in range(3)]
        fm = {nm: [sba("%s%d" % (nm, p), [128, 256], BF16) for p in range(3)]
              for nm in ("rT0", "rT1", "aT0", "aT1", "bT", "kT", "bh", "kh")}
        rkTp = [sba("rkT%d" % p, [128, 256], BF16) for p in range(3)]
        kk2 = sba("kk2", [128, 256], BF16)
        Bh_tm = [sba("Bh_tm%d" % p, [128, 256], BF16) for p in range(3)]
        Kh_tm = [sba("Kh_tm%d" % p, [128, 256], BF16) for p in range(3)]
        ZT = [[sba("ZT%d%d" % (p, i), [128, 512], BF16) for i in range(2)] for p in range(2)]
        SS = [[sba("SS%d%d" % (p, i), [128, 512], BF16) for i in range(2)] for p in range(2)]
        ArbT = [sba("ArbT%d" % p, [128, 512], BF16) for p in range(2)]
        AakT = [sba("AakT%d" % p, [128, 512], BF16) for p in range(2)]
        ArkT = [sba("ArkT%d" % p, [128, 512], BF16) for p in range(2)]
        Xb = [sba("Xb%d" % p, [128, 512], BF16) for p in range(2)]
        RpT = [sba("RpT%d" % p, [64, 512], BF16) for p in range(2)]
        Qh = [sba("Qh%d" % p, [64, 256]) for p in range(2)]
        PT = [sba("PT%d" % p, [64, 256]) for p in range(2)]
        Hb = [sba("Hb%d" % p, [64, 4, 64], BF16) for p in range(2)]
        Hst = sba("Hst", [64, 16, 64])
        ynbp = [sba("ynb%d" % p, [128, 256], BF16) for p in range(2)]
        gstp = [sba("gst%d" % p, [128, 2, 2, 4]) for p in range(2)]
        ygT = sba("ygT", [128, 8, 256], BF16)
        otmp = sba("otmp", [128, D])

        rec.add("pool", lambda e: e.memset(Hst[:, :, :], 0.0), [], [("Hst", h) for h in range(16)])
        rec.add("pool", lambda e: e.memset(xnT[:, :, 0:1], 0.0), [], [("xnT", dc) for dc in range(8)])

        class _Cut(Exception):
            pass

        def cut(k, buf_ap, keys, width=D):
            if DEBUG.get("cut") == k:
                dma("sp", out_d.rearrange("(n p) d -> p n d", p=128)[:, 0:2, 0:width], buf_ap, keys, ["cutout"])
                raise _Cut()

        TT0p = [sba("TT0%d" % p, [128, 256]) for p in range(2)]
        xres = sba("xres", [128, D])

        def pre_a(bi):
            pf = bi < 8
            tb = bi % 8
            cl = ["k", "v", "w", "a"] if pf else ["k", "v", "w", "a", "r", "g"]
            cidx = dict(r=0, k=1, v=2, w=3, a=4, g=5)
            src = (xp if pf else xo).rearrange("(n p) d -> p n d", p=128)[:, tb * 2:tb * 2 + 2, :]
            dma("sp", xin[:, :, :], src, [], [("xin", 0), ("xin", 1)])
            cut(1, xin[:, :, :], [("xin", 0), ("xin", 1)])
            rms_stats([xin[:, n, :] for n in range(2)], ssq, rstd, 2, [[("xin", n)] for n in range(2)], "nA")
            for n in range(2):
                act(xs_tm[:, n, :], xin[:, n, :], AF.Copy, [("xin", n), ("nA", "rstd")], [("xs_tm", n)], scale=rstd[:, n:n + 1])
            cut(2, xs_tm[:, :, :], [("xs_tm", 0), ("xs_tm", 1)])
            yield
            for dc in range(8):
                pb = ps[6 + dc % 2]
                for n in range(2):
                    mm(pb[:, n * 128:(n + 1) * 128], xs_tm[:, n, dc * 128:(dc + 1) * 128], identb[:, :], True, True,
                       [("xs_tm", n), "identb"], [("ps", 6 + dc % 2)])
                if dc % 2:
                    act(xnT[:, dc, 1:257], pb[:, 0:256], AF.Copy, [("ps", 6 + dc % 2), "pcol"], [("xnT", dc)],
                        scale=pc("g00", dc))
                else:
                    ts("dve", xnT[:, dc, 1:257], pb[:, 0:256], pc("g00", dc), None, ALU.mult, None,
                       [("ps", 6 + dc % 2), "pcol"], [("xnT", dc)])
            yield

        prefetched = set()

        def blk_pre(bi):
            pf = bi < 8
            tb = bi % 8
            cl = ["k", "v", "w", "a"] if pf else ["k", "v", "w", "a", "r", "g"]
            cidx = dict(r=0, k=1, v=2, w=3, a=4, g=5)
            if bi not in prefetched:
                yield from pre_a(bi)
            for dc in range(8):
                tt("dve", xx[:, dc, :], xnT[:, dc, 0:256], xnT[:, dc, 1:257], ALU.subtract, [("xnT", dc)], [("xx", dc)])

            def mix_into(buf, ci, keyf):
                for dc in range(8):
                    stt(buf[:, dc, :], xx[:, dc, :], pc("mix", ci * 8 + dc), xnT[:, dc, 1:257], ALU.mult, ALU.add,
                        [("xx", dc), ("xnT", dc), "pcol"], [keyf(dc)])

            mix_into(xsK, 1, lambda dc: ("xs", 1, dc))
            if not pf:
                mix_into(xsR, 0, lambda dc: ("xs", 0, dc))
            tk = lambda dc: ("xsT", dc)
            yield
            mix_into(xsT, 3, tk)
            pb = ps[6]
            for dc in range(8):
                mm(pb[0:64, 0:256], w1s[:, dc, :], xsT[:, dc, :], dc == 0, dc == 7, ["w1s", tk(dc)], [("ps", 6)])
            act(hid_w[:, :], pb[0:64, 0:256], AF.Tanh, [("ps", 6)], ["hid_w"])
            mix_into(xsT, 4, tk)
            pb = ps[7]
            for dc in range(8):
                mm(pb[0:64, 0:256], a1s[:, dc, :], xsT[:, dc, :], dc == 0, dc == 7, ["a1s", tk(dc)], [("ps", 7)])
            cp("dve", hid_a[:, :], pb[0:64, 0:256], [("ps", 7)], ["hid_a"])
            if not pf:
                mix_into(xsT, 5, tk)
                pb = ps[6]
                for dc in range(8):
                    mm(pb[:, 0:256], g1s[:, dc, 0:128], xsT[:, dc, :], dc == 0, dc == 7, ["g1s", tk(dc)], [("ps", 6)])
                for dc in range(8):
                    mm(pb[0:32, 256:512], g1s[:, dc, 128:160], xsT[:, dc, :], dc == 0, dc == 7, ["g1s", tk(dc)], [("ps", 6)])
                act(hid_g[:, :], pb[:, 0:256], AF.Sigmoid, [("ps", 6)], ["hid_g"])
                act(hid_g2[:, :], pb[0:32, 256:512], AF.Sigmoid, [("ps", 6)], ["hid_g2"])
            yield
            mix_into(xsT, 2, tk)
            for dc in range(8):
                cp("pool", xnT[:, dc, 0:1], xnT[:, dc, 256:257], [("xnT", dc), ("xx", dc), tk(dc), ("xs", 1, dc), ("xs", 0, dc)],
                   [("xnT", dc)])
            for n in range(2):
                for eh in range(2):
                    pb = ps[6 + eh]
                    for dc in range(8):
                        mm(pb[:, :], xsT[:, dc, n * 128:(n + 1) * 128], Wr[:, 2, dc, eh * 512:(eh + 1) * 512], dc == 0, dc == 7,
                           [("xsT", dc)] + Wrk(2), [("ps", 6 + eh)])
                    evac_copy(v_tm[:, n, eh * 512:(eh + 1) * 512], pb[:, :], [("ps", 6 + eh)], [("v_tm", n, eh)])
            vk = lambda n: [("v_tm", n, 0), ("v_tm", n, 1)]
            cut(4, xin[:, :, :], vk(0) + vk(1) + ["hid_w", "hid_a", "hid_g", "hid_g2"])

            yield

        def hp_work(bi, hp):
            pf = bi < 8
            tb = bi % 8
            cl = ["k", "v", "w", "a"] if pf else ["k", "v", "w", "a", "r", "g"]
            cidx = dict(r=0, k=1, v=2, w=3, a=4, g=5)
            vk = lambda n: [("v_tm", n, 0), ("v_tm", n, 1)]
            par = hp % 2
            pp = hp % 3
            DW, rkT, v_fm, y_tm, ysq, ynb, gst = DWp[pp], rkTp[pp], v_fmp[par], y_tmp[par], ysqp[par], ynbp[par], gstp[par]
            TT0, TT1 = TT0p[par], ysqp[par]
            hsl = slice(hp * 128, hp * 128 + 128)
            sgw, cs_, t2, E1, E2, E3, E4, asig, kkt, kkn, beta, kmod = T
            pb = ps[6]
            mm(pb[:, 0:256], w2s[:, hsl], hid_w[:, :], True, True, ["w2s", "hid_w"], [("ps", 6)])
            sigmoid_el(sgw[:, :], pb[:, 0:256], negc[:, hp:hp + 1], [("ps", 6)], ["sgw"])
            rec.add("dve", lambda e, a=cs_, b=sgw: e.tensor_tensor_scan(a[:, :], keepm, b[:, :], 0.0, ALU.mult, ALU.subtract),
                    ["sgw", "cst"], ["cs"])
            tt("dve", t2[:, :], cs_[:, :], sgw[:, :], ALU.add, ["cs", "sgw"], ["t2"])
            if not pf:
                act(E1[:, :], cs_[:, :], AF.Exp, ["cs"], ["E1"], scale=CDEC)
            act(E2[:, :], t2[:, :], AF.Exp, ["t2"], ["E2"], scale=CDEC)
            act(E3[:, :], cs_[:, :], AF.Exp, ["cs"], ["E3"], scale=-CDEC)
            ts("dve", csC[:, :], cs_[:, 127:256:128], CDEC, None, ALU.mult, None, ["cs"], ["csC"])
            act(WC[:, :], csC[:, :], AF.Exp, ["csC"], ["WC"])
            for c in range(2):
                act(E4[:, c * 128:(c + 1) * 128], cs_[:, c * 128:(c + 1) * 128], AF.Exp, ["cs", "csC"], [("E4", c)],
                    scale=-CDEC, bias=csC[:, c:c + 1])
                ts("pool", DW[:, c, :], ifold, WC[:, c:c + 1], None, ALU.mult, None, ["cst", "WC"], [("DW", pp, c)])
            cut(50, xin[:, :, :], ["sgw"])
            cut(51, xin[:, :, :], ["cs"])
            cut(52, xin[:, :, :], ["t2", "E1", "E2", "E3"])
            cut(53, xin[:, :, :], ["WC", ("E4", 0), ("E4", 1), ("DW", pp, 0), ("DW", pp, 1)])
            yield
            pb = ps[7]
            mm(pb[:, 0:256], a2s[:, hsl], hid_a[:, :], True, True, ["a2s", "hid_a"], [("ps", 7)])
            sigmoid_el(asig[:, :], pb[:, 0:256], negc[:, 8 + hp:9 + hp], [("ps", 7)], ["asig"])
            yield
            pk = ps[6]
            for dc in range(8):
                mm(pk[:, 0:256], Wr[:, 1, dc, hsl], xsK[:, dc, :], dc == 0, dc == 7, Wrk(1) + [("xs", 1, dc)], [("ps", 6)])
            ts("dve", kkt[:, :], pk[:, 0:256], pc("kk", hp), None, ALU.mult, None, [("ps", 6), "pcol"], ["kkt"])
            tt("dve", kk2[:, :], kkt[:, :], kkt[:, :], ALU.mult, ["kkt"], ["kk2"])
            pn = ps[7]
            mm(pn[:, 256:512], bonesb[:, :], kk2[:, :], True, True, ["bonesb", "kk2"], [("ps", 7)])
            rsqrt(kkn[:, :], pn[:, 256:512], 1.0, 1, [("ps", 7)], ["kkn"], "kkn")
            tt("dve", kkn[:, :], kkn[:, :], kkt[:, :], ALU.mult, ["kkn", "kkt"], ["kkn"])
            cut(54, xin[:, :, :], ["kkn", "asig"])
            tt("dve", beta[:, :], kkn[:, :], asig[:, :], ALU.mult, ["kkn", "asig"], ["beta"])
            for hh_ in range(2):
                stt(fm["aT%d" % hh_][pp][:, :], kkn[:, :], pcol[:, PC["hm"] + 2 + hh_:PC["hm"] + 3 + hh_], E2[:, :], ALU.mult, ALU.mult,
                    ["kkn", "E2", "pcol"], [("aT", pp, hh_)])
            tt("pool", fm["bT"][pp][:, :], beta[:, :], E3[:, :], ALU.mult, ["beta", "E3"], [("bT", pp)])
            tt("pool", fm["bh"][pp][:, :], beta[:, :], E4[:, :], ALU.mult, ["beta", ("E4", 0), ("E4", 1)], [("bh", pp)])
            yield
            ts("dve", t2[:, :], asig[:, :], pc("ka", hp), pc("ka", hp), ALU.mult, ALU.subtract, ["asig", "pcol"], ["t2"])
            stt(kmod[:, :], t2[:, :], 1.0, pk[:, 0:256], ALU.add, ALU.mult, ["t2", ("ps", 6)], ["kmod"])
            tt("pool", fm["kT"][pp][:, :], kmod[:, :], E3[:, :], ALU.mult, ["kmod", "E3"], [("kT", pp)])
            tt("pool", fm["kh"][pp][:, :], kmod[:, :], E4[:, :], ALU.mult, ["kmod", ("E4", 0), ("E4", 1)], [("kh", pp)])
            if not pf:
                pr = ps[7]
                for dc in range(8):
                    mm(pr[:, 0:256], Wr[:, 0, dc, hsl], xsR[:, dc, :], dc == 0, dc == 7, Wrk(0) + [("xs", 0, dc)], [("ps", 7)])
                stt(rkT[:, :], pr[:, 0:256], pc("rk", hp), kmod[:, :], ALU.mult, ALU.mult, [("ps", 7), "pcol", "kmod"], [("rkT", pp)])
                for hh_ in range(2):
                    stt(fm["rT%d" % hh_][pp][:, :], pr[:, 0:256], pcol[:, PC["hm"] + hh_:PC["hm"] + 1 + hh_], E1[:, :], ALU.mult, ALU.mult,
                        [("ps", 7), "E1", "pcol"], [("rT", pp, hh_)])
            cut(55, xin[:, :, :], [("rT", pp, 0), ("aT", pp, 0), ("aT", pp, 1), ("bT", pp), ("kT", pp), ("rkT", pp), ("bh", pp), ("kh", pp)])
            yield
            pb = ps[7]
            for c in range(2):
                mm(pb[:, c * 128:(c + 1) * 128], fm["bh"][pp][:, c * 128:(c + 1) * 128], identb[:, :], True, True,
                   [("bh", pp), "identb"], [("ps", 7)])
                mm(pb[:, 256 + c * 128:256 + (c + 1) * 128], fm["kh"][pp][:, c * 128:(c + 1) * 128], identb[:, :], True, True,
                   [("kh", pp), "identb"], [("ps", 7)])
            cut(56, xin[:, :, :], [("ps", 7)])
            if DEBUG.get("var") == "bh_dve":
                cp("dve", Bh_tm[pp][:, :], pb[:, 0:256], [("ps", 7)], [("Bh_tm", pp)])
            else:
                act(Bh_tm[pp][:, :], pb[:, 0:256], AF.Copy, [("ps", 7)], [("Bh_tm", pp)])
            cp("dve", Kh_tm[pp][:, :], pb[:, 256:512], [("ps", 7)], [("Kh_tm", pp)])

            cut(5, xin[:, :, :], [("Bh_tm", pp), ("Kh_tm", pp), ("rT", pp, 0), ("aT", pp, 0), ("aT", pp, 1), ("bT", pp), ("kT", pp), ("rkT", pp), ("DW", pp, 0), ("DW", pp, 1)])
            yield "prep_done"
            b0, b1, b2 = ps[par * 3], ps[par * 3 + 1], ps[par * 3 + 2]
            k0, k1, k2 = ("ps", par * 3), ("ps", par * 3 + 1), ("ps", par * 3 + 2)
            aTm = [fm["aT0"][pp], fm["aT1"][pp]]
            rTm = [fm["rT0"][pp], fm["rT1"][pp]]
            bT_, kT_ = fm["bT"][pp], fm["kT"][pp]
            slots = [(hh, c) for hh in range(2) for c in range(2)]

            def sl(q):
                return slice(q * 128, (q + 1) * 128)

            def csl(c):
                return slice(c * 128, (c + 1) * 128)

            for q, (hh, c) in enumerate(slots):
                mm(b0[:, sl(q)], bT_[:, csl(c)], aTm[hh][:, csl(c)], True, True, [("bT", pp), ("aT", pp, hh)], [k0])
                mm(b1[:, sl(q)], aTm[hh][:, csl(c)], bT_[:, csl(c)], True, True, [("bT", pp), ("aT", pp, hh)], [k1])
                mm(b2[:, sl(q)], kT_[:, csl(c)], aTm[hh][:, csl(c)], True, True, [("kT", pp), ("aT", pp, hh)], [k2])
            yield
            cut(61, xin[:, :, :], [k0, k1, k2])
            tt("dve", ZT[par][0][:, :], b0[:, :], mSL, ALU.mult, [k0, "maskb"], [("ZT", par, 0)])
            tt("dve", SS[par][0][:, :], b1[:, :], mGT, ALU.mult, [k1, "maskb"], [("SS", par, 0)])
            tt("dve", AakT[par][:, :], b2[:, :], mSL, ALU.mult, [k2, "maskb"], [("AakT", par)])
            if not pf:
                for q, (hh, c) in enumerate(slots):
                    mm(b0[:, sl(q)], bT_[:, csl(c)], rTm[hh][:, csl(c)], True, True, [("bT", pp), ("rT", pp, hh)], [k0])
                    mm(b1[:, sl(q)], kT_[:, csl(c)], rTm[hh][:, csl(c)], True, True, [("kT", pp), ("rT", pp, hh)], [k1])
                tt("dve", ArbT[par][:, :], b0[:, :], mLE, ALU.mult, [k0, "maskb"], [("ArbT", par)])
                tt("dve", ArkT[par][:, :], b1[:, :], mLE, ALU.mult, [k1, "maskb"], [("ArkT", par)])
            cut(62, xin[:, :, :], [("ZT", par, 0), ("SS", par, 0), ("AakT", par), ("ArbT", par), ("ArkT", par)])
            yield
            for q, (hh, c) in enumerate(slots):
                hc = slice((hp * 2 + hh) * 64, (hp * 2 + hh) * 64 + 64)
                mm(b2[:, q * 128:q * 128 + 64], AakT[par][:, sl(q)], v_tm[:, c, hc], True, True,
                   [("AakT", par)] + vk(c), [k2])
                mm(b2[:, q * 128 + 64:(q + 1) * 128], aTm[hh][:, csl(c)], ifoldb[:, :],
                   True, True, [("aT", pp, hh), "ifoldb"], [k2])
            cut(63, xin[:, :, :], [k2])
            evac_copy(Xb[par][:, :], b2[:, :], [k2], [("Xb", par)])
            cut(64, xin[:, :, :], [("Xb", par)])
            for k in range(7):
                zi, zo = k % 2, (k + 1) % 2
                if k < 5:
                    for q in range(4):
                        mm(b0[:, sl(q)], ZT[par][zi][:, sl(q)], SS[par][zi][:, sl(q)], True, True,
                           [("ZT", par, zi), ("SS", par, zi)], [k0])
                    for q in range(4):
                        mm(b1[:, sl(q)], SS[par][zi][:, sl(q)], ZT[par][zi][:, sl(q)], True, True,
                           [("ZT", par, zi), ("SS", par, zi)], [k1])
                for q in range(4):
                    mm(b2[:, sl(q)], ZT[par][zi][:, sl(q)], Xb[par][:, sl(q)], True, True,
                       [("ZT", par, zi), ("Xb", par)], [k2])
                if k < 5:
                    act(SS[par][zo][:, :], b0[:, :], AF.Copy, [k0], [("SS", par, zo)])
                if k < 6:
                    act(ZT[par][zo][:, :], b1[:, :], AF.Copy, [k1], [("ZT", par, zo)])
                tt("dve", Xb[par][:, :], b2[:, :], Xb[par][:, :], ALU.add, [k2, ("Xb", par)], [("Xb", par)])
                yield
            UA = Xb[par]
            cut(6, xin[:, :, :], [("Xb", par), ("ArbT", par), ("ArkT", par)])
            yield
            if not pf:
                for q, (hh, c) in enumerate(slots):
                    mm(b0[0:64, sl(q)], UA[:, q * 128 + 64:(q + 1) * 128], ArbT[par][:, sl(q)], True, False,
                       [("Xb", par), ("ArbT", par)], [k0])
                    mm(b0[0:64, sl(q)], ifoldb[:, :], rTm[hh][:, csl(c)], False, True,
                       ["ifoldb", ("rT", pp, hh)], [k0])
                evac_copy(RpT[par][:, :], b0[0:64, :], [k0], [("RpT", par)])
            for q, (hh, c) in enumerate(slots):
                hc = slice((hp * 2 + hh) * 64, (hp * 2 + hh) * 64 + 64)
                mm(b1[0:64, q * 64:(q + 1) * 64], Bh_tm[pp][:, c * 128 + hh * 64:c * 128 + (hh + 1) * 64], UA[:, q * 128:q * 128 + 64], True, False,
                   [("Bh_tm", pp), ("Xb", par)], [k1])
                mm(b1[0:64, q * 64:(q + 1) * 64], Kh_tm[pp][:, c * 128 + hh * 64:c * 128 + (hh + 1) * 64], v_tm[:, c, hc], False, True,
                   [("Kh_tm", pp)] + vk(c), [k1])
                mm(b1[0:64, 256 + q * 64:256 + (q + 1) * 64], UA[:, q * 128 + 64:(q + 1) * 128], Bh_tm[pp][:, c * 128 + hh * 64:c * 128 + (hh + 1) * 64],
                   True, False, [("Bh_tm", pp), ("Xb", par)], [k1])
                mm(b1[0:64, 256 + q * 64:256 + (q + 1) * 64], self32[:, hh * 64:(hh + 1) * 64], DW[:, c, :], False, True,
                   ["cst", ("DW", pp, c)], [k1])
            cp("dve", Qh[par][:, :], b1[0:64, 0:256], [k1], [("Qh", par)])
            act(PT[par][:, :], b1[0:64, 256:512], AF.Copy, [k1], [("PT", par)])
            cut(7, xin[:, :, :], [("Qh", par), ("PT", par), ("RpT", par)])
            yield
            for q, (hh, c) in enumerate(slots):
                hd = hp * 2 + hh
                if not pf:
                    act(Hb[par][:, q, :], Hst[:, hd, :], AF.Copy, [("Hst", hd)], [("Hb", par, q)])
                mm(b2[0:64, q * 64:(q + 1) * 64], PT[par][:, q * 64:(q + 1) * 64], Hst[:, hd, :], True, True, [("PT", par), ("Hst", hd)], [k2])
                tt("dve", Hst[:, hd, :], b2[0:64, q * 64:(q + 1) * 64], Qh[par][:, q * 64:(q + 1) * 64], ALU.add, [k2, ("Qh", par)], [("Hst", hd)])
                yield
            yield
            if not pf:
                for q, (hh, c) in enumerate(slots):
                    hc = slice((hp * 2 + hh) * 64, (hp * 2 + hh) * 64 + 64)
                    oc = slice(c * 128 + hh * 64, c * 128 + hh * 64 + 64)
                    mm(b0[:, oc], RpT[par][:, sl(q)], Hb[par][:, q, :], True, False, [("RpT", par), ("Hb", par, q)], [k0])
                    mm(b0[:, oc], ArbT[par][:, sl(q)], UA[:, q * 128:q * 128 + 64], False, False, [("ArbT", par), ("Xb", par)], [k0])
                    mm(b0[:, oc], ArkT[par][:, sl(q)], v_tm[:, c, hc], False, True, [("ArkT", par)] + vk(c), [k0])
                evac_copy(y_tm[:, :], b0[:, 0:256], [k0], [("y_tm", par)])
                if stage == 1:
                    dma("sp", out_d.rearrange("(n p) d -> p n d", p=128)[:, tb * 2:tb * 2 + 2, hsl], y_tm[:, :].rearrange("p (c k) -> p c k", c=2),
                        [("y_tm", par)], [("outblk", tb, hp)])
                    return
                yield
                yv = y_tm[:, :].rearrange("p (a k) -> p a k", k=64)
                g4 = gst[:, :, :, :].rearrange("p n h s -> p (n h) s")
                rec.add("dve", lambda e, o=g4[:, :, 0], i=yv: e.tensor_reduce(o, i, mybir.AxisListType.X, ALU.add),
                        [("y_tm", par)], [("gst", par, 0)])
                tt("dve", ysq[:, :], y_tm[:, :], y_tm[:, :], ALU.mult, [("y_tm", par)], [("ysq", par)])
                rec.add("dve", lambda e, o=g4[:, :, 1], i=ysq[:, :].rearrange("p (a k) -> p a k", k=64):
                        e.tensor_reduce(o, i, mybir.AxisListType.X, ALU.add), [("ysq", par)], [("gst", par, 1)])
                ts("dve", g4[:, :, 0], g4[:, :, 0], 1.0 / 64, None, ALU.mult, None, [("gst", par, 0)], [("gst", par, 0)])
                tt("dve", g4[:, :, 2], g4[:, :, 0], g4[:, :, 0], ALU.mult, [("gst", par, 0)], [("gst", par, 2)])
                stt(g4[:, :, 3], g4[:, :, 1], 1.0 / 64, g4[:, :, 2], ALU.mult, ALU.subtract,
                    [("gst", par, 1), ("gst", par, 2)], [("gst", par, 3)])
                rsqrt(g4[:, :, 3], g4[:, :, 3], 1.0, 2, [("gst", par, 3)], [("gst", par, 3)], "gst")
                tt("dve", ysq[:, :].rearrange("p (a k) -> p a k", k=64), yv,
                   g4[:, :, 0:1].to_broadcast([128, 4, 64]), ALU.subtract, [("y_tm", par), ("gst", par, 0), ("ysq", par)], [("ysq", par)])
                tt("dve", ynb[:, :].rearrange("p (a k) -> p a k", k=64), ysq[:, :].rearrange("p (a k) -> p a k", k=64),
                   g4[:, :, 3:4].to_broadcast([128, 4, 64]), ALU.mult, [("ysq", par), ("gst", par, 3)], [("ynb", par)])
                yield
                pa, pb2 = b1, b2
                for n in range(2):
                    mm(pa[:, n * 128:(n + 1) * 128], ynb[:, n * 128:(n + 1) * 128], identb[:, :], True, True, [("ynb", par), "identb"], [k1])
                    mm(pb2[:, 256 + n * 128:256 + (n + 1) * 128], v_tm[:, n, hsl], identb[:, :], True, True, vk(n) + ["identb"], [k2])
                mm(pa[:, 256:512], bonesb[:, :], rkT[:, :], True, True, ["bonesb", ("rkT", pp)], [k1])
                mm(pb2[:, 0:256], g2a[:, hsl], hid_g[:, :], True, False, ["g2a", "hid_g"], [k2])
                mm(pb2[:, 0:256], g2b[:, hsl], hid_g2[:, :], False, True, ["g2b", "hid_g2"], [k2])
                t0, t1 = TT0, TT1
                act(v_fm[:, :], pb2[:, 256:512], AF.Copy, [k2], [("v_fm", par)])
                ts("dve", t0[:, :], pa[:, 0:256], pc("gnw", hp), pc("gnb", hp), ALU.mult, ALU.add, [k1, "pcol"], [("tt0", par)])
                tt("dve", t1[:, :], pa[:, 256:512], v_fm[:, :], ALU.mult, [k1, ("v_fm", par)], [("ysq", par)])
                tt("dve", t0[:, :], t0[:, :], t1[:, :], ALU.add, [("tt0", par), ("ysq", par)], [("tt0", par)])
                tt("dve", ygT[:, hp, :], pb2[:, 0:256], t0[:, :], ALU.mult, [k2, ("tt0", par)], [("ygT", hp)])

            yield

        dstall = (out_d if stage == 2 else x1s).rearrange("(n p) d -> p n d", p=128)

        def blk_post(bi):
            pf = bi < 8
            tb = bi % 8
            cl = ["k", "v", "w", "a"] if pf else ["k", "v", "w", "a", "r", "g"]
            cidx = dict(r=0, k=1, v=2, w=3, a=4, g=5)
            vk = lambda n: [("v_tm", n, 0), ("v_tm", n, 1)]
            for n in range(2):
                for eh in range(2):
                    pb = ps[6 + eh]
                    for cc in range(8):
                        mm(pb[:, :], ygT[:, cc, n * 128:(n + 1) * 128], Wo[:, cc, eh * 512:(eh + 1) * 512], cc == 0, cc == 7,
                           [("ygT", cc)] + Wok, [("ps", 6 + eh)])
                for eh in range(2):
                    act(junk[:, 0:512], ps[6 + eh][:, :], AF.Square, [("ps", 6 + eh)], ["junk", ("pn", "ssq", eh)], accum=ssq[:, 2 + eh:3 + eh])
                tt("dve", rstd[:, 2:3], ssq[:, 2:3], ssq[:, 3:4], ALU.add, [("pn", "ssq", 0), ("pn", "ssq", 1)], [("pn", "rstd")])
                rsqrt(rstd[:, 2:3], rstd[:, 2:3], 1.0 / D, 0, [("pn", "rstd")], [("pn", "rstd")], "pn")
                for eh in range(2):
                    stt(otmp[:, eh * 512:(eh + 1) * 512], ps[6 + eh][:, :], rstd[:, 2:3], growb[:, eh * 512:(eh + 1) * 512], ALU.mult, ALU.mult,
                        [("ps", 6 + eh), ("pn", "rstd"), "growb"], ["otmp"])
                dma("sp", xres[:, :], xo.rearrange("(n p) d -> p n d", p=128)[:, tb * 2 + n, :], [], ["xres"])
                tt("dve", xres[:, :], xres[:, :], otmp[:, :], ALU.add, ["xres", "otmp"], ["xres"])
                dma("sp", dstall[:, tb * 2 + n, :], xres[:, :], ["xres"], [("x1s", tb, n)])


            yield

        def drive(gens, depth):
            active = []
            pend = list(gens)
            prev_ready = True
            while active or pend:
                if pend and len(active) < depth and prev_ready:
                    g_ = pend.pop(0)
                    active.append({"g": g_[0], "st": "prep", "hook": g_[1]} if isinstance(g_, tuple) else {"g": g_, "st": "prep"})
                    prev_ready = False
                for ent in list(active):
                    if ent["st"] == "wait":
                        if sum(1 for e in active if e["st"] == "scan") < 2:
                            ent["st"] = "scan"
                        else:
                            continue
                    try:
                        v = next(ent["g"])
                        if v == "prep_done":
                            ent["st"] = "wait"
                            prev_ready = True
                            if ent.get("hook"):
                                ent["hook"]()
                    except StopIteration:
                        if ent["st"] == "prep":
                            prev_ready = True
                        active.remove(ent)

        def run_all(g):
            for _ in g:
                pass

        try:
            for bi in range(16):
                if bi not in DEBUG["blocks"]:
                    continue
                run_all(blk_pre(bi))
                nxt = bi + 1

                def hook(nxt=nxt):
                    if nxt < 16 and nxt in DEBUG["blocks"] and DEBUG.get("prefetch", True):
                        run_all(pre_a(nxt))
                        prefetched.add(nxt)

                drive([(hp_work(bi, hp), hook) if hp == 5 else hp_work(bi, hp) for hp in range(8)], DEBUG.get("depth", 3))
                if bi >= 8 and stage != 1:
                    run_all(blk_post(bi))
        except _Cut:
            sched.flush(rec, final=True)
            return nc, st
        sched.flush(rec, final=(stage <= 2))
    if stage <= 2:
        return nc, st

    x2s = nc.dram_tensor("x2s", [NTOK, D], F32, kind="Internal").ap()

    def tiles(ap):
        return ap.rearrange("(n p) d -> p n d", p=128)

    def norm_to_fm(sp_, src, hT, gname, xt4, xs4, ssq16, rstd16):
        for g in range(8):
            xb_, sq_ = xt4[g % 2], xs4[g % 2]
            gk = g % 2
            dma("sp", xb_[:, :, :], tiles(src)[:, g * 2:(g + 1) * 2, :], [], [("xt4", gk, n) for n in range(2)])
            for n in range(2):
                act(junk[:, :], xb_[:, n, :], AF.Square, [("xt4", gk, n)], ["junk", ("ni", "ssq", gk)], accum=ssq16[:, gk * 2 + n:gk * 2 + n + 1])
            rsqrt(rstd16[:, gk * 2:gk * 2 + 2], ssq16[:, gk * 2:gk * 2 + 2], 1.0 / D, 0, [("ni", "ssq", gk)], [("ni", "rstd", gk)], "ni")
            for n in range(2):
                if n % 2:
                    act(sq_[:, n, :], xb_[:, n, :], AF.Copy, [("xt4", gk, n), ("ni", "rstd", gk)], [("xs4", gk, n)],
                        scale=rstd16[:, gk * 2 + n:gk * 2 + n + 1])
                else:
                    ts("dve", sq_[:, n, :], xb_[:, n, :], rstd16[:, gk * 2 + n:gk * 2 + n + 1], None, ALU.mult, None,
                       [("xt4", gk, n), ("ni", "rstd", gk)], [("xs4", gk, n)])
            for dc in range(8):
                pb = ps[6 + dc % 2]
                for n in range(2):
                    mm(pb[:, n * 128:(n + 1) * 128], sq_[:, n, dc * 128:(dc + 1) * 128], identb[:, :], True, True,
                       [("xs4", gk, n), "identb"], [("ps", 6 + dc % 2)])
                if dc % 2:
                    act(hT[:, dc, g * 256:(g + 1) * 256], pb[:, 0:256], AF.Copy, [("ps", 6 + dc % 2), "pcol"], [("hT", dc, g // 2)],
                        scale=pc(gname, dc))
                else:
                    ts("dve", hT[:, dc, g * 256:(g + 1) * 256], pb[:, 0:256], pc(gname, dc), None, ALU.mult, None,
                       [("ps", 6 + dc % 2), "pcol"], [("hT", dc, g // 2)])

    def post_tail(pA, pB, kA, kB, src, dst, n, xt, otmp_, growb_, ssq_, rstd_, p=0):
        c0 = 4 + 2 * p
        kx = ("xt4", p, 0)
        dma("sp", xt[:, :], tiles(src)[:, n, :], [], [kx])
        act(junk[:, 0:512], pA[:, :], AF.Square, [kA], ["junk", ("pt", p, "s0")], accum=ssq_[:, c0:c0 + 1])
        act(junk[:, 512:1024], pB[:, :], AF.Square, [kB], ["junk", ("pt", p, "s1")], accum=ssq_[:, c0 + 1:c0 + 2])
        tt("dve", rstd_[:, c0:c0 + 1], ssq_[:, c0:c0 + 1], ssq_[:, c0 + 1:c0 + 2], ALU.add, [("pt", p, "s0"), ("pt", p, "s1")], [("pt", p, "rstd")])
        rsqrt(rstd_[:, c0:c0 + 1], rstd_[:, c0:c0 + 1], 1.0 / D, 0, [("pt", p, "rstd")], [("pt", p, "rstd")], "pt")
        stt(otmp_[:, 0:512], pA[:, :], rstd_[:, c0:c0 + 1], growb_[:, 0:512], ALU.mult, ALU.mult, [kA, ("pt", p, "rstd"), "growb"], [("otmp", p)])
        stt(otmp_[:, 512:1024], pB[:, :], rstd_[:, c0:c0 + 1], growb_[:, 512:1024], ALU.mult, ALU.mult, [kB, ("pt", p, "rstd"), "growb"], [("otmp", p)])
        tt("dve", xt[:, :], xt[:, :], otmp_[:, :], ALU.add, [kx, ("otmp", p)], [kx])
        dma("sp", tiles(dst)[:, n, :], xt[:, :], [kx], [("dst", n)])

    def ffn_phase(l, src, dst, gname, gout_idx, final):
        with ExitStack() as sf:
            def sbf(name, shape, dt=F32):
                return sf.enter_context(nc.sbuf_tensor("sf%d_%s" % (l, name), list(shape), dt))
            hT = sbf("hT", [128, 8, NTOK], BF16)
            Wd = sbf("Wd", [128, NF, D], BF16)
            actb = sbf("act", [128, NF, 1024], BF16)
            wg = [sbf("wg%d" % i, [128, 8, 256], BF16) for i in range(3)]
            wu = [sbf("wu%d" % i, [128, 8, 256], BF16) for i in range(3)]
            xt4 = [sbf("xt4%d" % i, [128, 2, D]) for i in range(2)]
            xs4 = [sbf("xs4%d" % i, [128, 2, D], BF16) for i in range(2)]
            sil = [sbf("sil%d" % i, [128, 512]) for i in range(2)]
            growb_ = sbf("growb", [128, D])
            otmp_ = [sbf("otmp%d" % i, [128, D]) for i in range(2)]
            ssq_ = sbf("ssq", [128, 8])
            rstd_ = sbf("rstd", [128, 8])
            dma("sp", growb_[:, :], brow_d[gout_idx], [], ["growb"])
            for i, (f0, f1) in enumerate([(0, 6), (6, 12), (12, 17), (17, 22)]):
                dma("pool", Wd[:, f0:f1, :], ffd_d[l].rearrange("(f p) e -> p f e", p=128)[:, f0:f1, :], [], [("Wd", i)])
            Wdk = [("Wd", i) for i in range(4)]
            norm_to_fm(None, src, hT, gname, xt4, xs4, ssq_, rstd_)
            hk = lambda tb: [("hT", dc, tb) for dc in range(8)]
            it = 0
            for half in range(2):
                for f in range(NF):
                    bi_ = (f // 2) % 3
                    fo = (f % 2) * 128
                    if f % 2 == 0:
                        dma("pool", wg[bi_][:, :, :], ffg_d[l].rearrange("(k p) f -> p k f", p=128)[:, :, f * 128:(f + 2) * 128], [], [("wg", bi_)])
                        dma("pool", wu[bi_][:, :, :], ffu_d[l].rearrange("(k p) f -> p k f", p=128)[:, :, f * 128:(f + 2) * 128], [], [("wu", bi_)])
                    for tb in range(2):
                        gtb = half * 2 + tb
                        pg, pu = ps[(it % 2) * 2], ps[(it % 2) * 2 + 1]
                        kg, ku = ("ps", (it % 2) * 2), ("ps", (it % 2) * 2 + 1)
                        sl_ = sil[it % 2]
                        it += 1
                        for dc in range(8):
                            mm(pg[:, :], wg[bi_][:, dc, fo:fo + 128], hT[:, dc, gtb * 512:(gtb + 1) * 512], dc == 0, dc == 7,
                               [("wg", bi_), ("hT", dc, gtb)], [kg])
                        for dc in range(8):
                            mm(pu[:, :], wu[bi_][:, dc, fo:fo + 128], hT[:, dc, gtb * 512:(gtb + 1) * 512], dc == 0, dc == 7,
                               [("wu", bi_), ("hT", dc, gtb)], [ku])
                        act(sl_[:, :], pg[:, :], AF.Silu, [kg], [("sil", it % 2)])
                        tt("dve", actb[:, f, tb * 512:(tb + 1) * 512], pu[:, :], sl_[:, :], ALU.mult, [ku, ("sil", it % 2)], [("act", f, tb)])
                for tl in range(8):
                    n = half * 8 + tl
                    pA, pB = ps[4 + (tl % 2) * 2], ps[5 + (tl % 2) * 2]
                    kA, kB = ("ps", 4 + (tl % 2) * 2), ("ps", 5 + (tl % 2) * 2)
                    for f in range(NF):
                        mm(pA[:, :], actb[:, f, tl * 128:(tl + 1) * 128], Wd[:, f, 0:512], f == 0, f == NF - 1,
                           [("act", f, tl // 4)] + Wdk, [kA])
                    for f in range(NF):
                        mm(pB[:, :], actb[:, f, tl * 128:(tl + 1) * 128], Wd[:, f, 512:1024], f == 0, f == NF - 1,
                           [("act", f, tl // 4)] + Wdk, [kB])
                    post_tail(pA, pB, kA, kB, src, dst, n, xt4[tl % 2][:, 0, :], otmp_[tl % 2], growb_, ssq_, rstd_, p=tl % 2)
            sched.flush(rec, final=final)

    def sgu_phase(src, dst):
        with ExitStack() as sf:
            def sbf(name, shape, dt=F32):
                return sf.enter_context(nc.sbuf_tensor("sg_%s" % name, list(shape), dt))
            hT = sbf("hT", [128, 8, NTOK], BF16)
            Win = sbf("Win", [128, 8, 2 * D], BF16)
            Wout = sbf("Wout", [128, 8, D], BF16)
            uT = sbf("uT", [128, 8, NTOK], BF16)
            ws_sb = sbf("ws_sb", [128, 8, 128], BF16)
            wsT = sbf("wsT", [128, 8, 128], BF16)
            Cg = sbf("Cg", [128, 8, 128])
            onesb = sbf("onesb", [128, 128], BF16)
            rowb = sbf("rowb", [1, 2048 + 128], BF16)
            xt4 = [sbf("xt4%d" % i, [128, 2, D]) for i in range(2)]
            xs4 = [sbf("xs4%d" % i, [128, 2, D], BF16) for i in range(2)]
            vgp = [sbf("vg%d" % i, [128, D]) for i in range(2)]
            zbp = [sbf("zb%d" % i, [128, D], BF16) for i in range(2)]
            tmpc = sbf("tmpc", [128, 512])
            tmpcp = [sbf("tmpct%d" % i, [128, 512]) for i in range(2)]
            growb_ = sbf("growb", [128, D])
            otmp_ = [sbf("otmp%d" % i, [128, D]) for i in range(2)]
            ssq_ = sbf("ssq", [128, 8])
            rstd_ = sbf("rstd", [128, 8])
            st4p = [sbf("st4%d" % i, [128, 8]) for i in range(2)]
            dma("sp", growb_[:, :], brow_d[2], [], ["growb"])
            for q4 in range(4):
                dma("pool", Win[:, q4 * 2:(q4 + 1) * 2, :], sgu_in_d.rearrange("(k p) e -> p k e", p=128)[:, q4 * 2:(q4 + 1) * 2, :], [], [("Win", q4)])
            Wink = [("Win", q4) for q4 in range(4)]
            for hf in range(2):
                dma("pool", Wout[:, hf * 4:(hf + 1) * 4, :], sgu_out_d.rearrange("(k p) e -> p k e", p=128)[:, hf * 4:(hf + 1) * 4, :], [], [("Wout", hf)])
            Woutk = [("Wout", 0), ("Wout", 1)]
            dma("pool", ws_sb[:, :, :], sgu_ws_d.rearrange("g i j -> i g j"), [], ["ws_sb"])
            dma("pool", rowb[:, :], rowv_d[:, :], [], ["rowb"])
            rec.add("pool", lambda e: e.memset(onesb[:, :], 1.0), [], ["onesb"])
            for g in range(8):
                pb = ps[6]
                mm(pb[:, 0:128], ws_sb[:, g, :], identb[:, :], True, True, ["ws_sb", "identb"], [("ps", 6)])
                cp("dve", wsT[:, g, :], pb[:, 0:128], [("ps", 6)], [("wsT", g)])
                rec.add("pool", lambda e, g=g: e.memset(wsT[64:128, g, 0:64], 0.0), [("wsT", g)], [("wsT", g)])
                mm(pb[:, 128:256], onesb[:, :], wsT[:, g, :], True, True, ["onesb", ("wsT", g)], [("ps", 6)])
                mm(pb[:, 256:384], onesb[0:1, :], rowb[0:1, 1024 + g * 128:1024 + (g + 1) * 128], True, True, ["onesb", "rowb"], [("ps", 6)])
                cp("dve", tmpc[:, 0:128], pb[:, 256:384], [("ps", 6)], ["tmpc"])
                stt(Cg[:, g, :], pb[:, 128:256], pc("lnb", g), tmpc[:, 0:128], ALU.mult, ALU.add, [("ps", 6), "pcol", "tmpc"], [("Cg", g)])
            norm_to_fm(None, src, hT, "g10", xt4, xs4, ssq_, rstd_)
            it = 0
            for g in range(8):
                for tb in range(4):
                    pb = ps[it % 2]
                    kb = ("ps", it % 2)
                    it += 1
                    for dc in range(8):
                        mm(pb[:, :], Win[:, dc, g * 128:(g + 1) * 128], hT[:, dc, tb * 512:(tb + 1) * 512], dc == 0, dc == 7,
                           Wink + [("hT", dc, tb)], [kb])
                    act(uT[:, g, tb * 512:(tb + 1) * 512], pb[:, :], AF.Gelu, [kb, "pcol"], [("uT", g, tb * 4 + i_) for i_ in range(4)], bias=pc("binu", g))
            def sgu_tile(n):
                tb = n // 4
                p = n % 2
                vg, zb, tmpct, st4 = vgp[p], zbp[p], tmpcp[p], st4p[p]
                kv = [("ps", 4 * p), ("ps", 4 * p + 1)]
                pv = [ps[4 * p], ps[4 * p + 1]]
                for eh in range(2):
                    for dc in range(8):
                        mm(pv[eh][:, :], hT[:, dc, n * 128:(n + 1) * 128], Win[:, dc, D + eh * 512:D + (eh + 1) * 512], dc == 0, False,
                           Wink + [("hT", dc, tb)], [kv[eh]])
                    mm(pv[eh][:, :], onesb[0:1, :], rowb[0:1, eh * 512:(eh + 1) * 512], False, True, ["onesb", "rowb"], [kv[eh]])
                    act(vg[:, eh * 512:(eh + 1) * 512], pv[eh][:, :], AF.Gelu, [kv[eh]], [("vg", p, eh)], accum=st4[:, eh:eh + 1])
                    yield
                act(junk[:, :], vg[:, :], AF.Square, [("vg", p, 0), ("vg", p, 1)], ["junk", ("st4", p, 2)], accum=st4[:, 2:3])
                tt("dve", st4[:, 3:4], st4[:, 0:1], st4[:, 1:2], ALU.add, [("vg", p, 0), ("vg", p, 1)], [("st4", p, 3)])
                ts("dve", st4[:, 3:4], st4[:, 3:4], 1.0 / D, None, ALU.mult, None, [("st4", p, 3)], [("st4", p, 3)])
                tt("dve", st4[:, 5:6], st4[:, 3:4], st4[:, 3:4], ALU.mult, [("st4", p, 3)], [("st4", p, 5)])
                stt(st4[:, 4:5], st4[:, 2:3], 1.0 / D, st4[:, 5:6], ALU.mult, ALU.subtract, [("st4", p, 2), ("st4", p, 5)], [("st4", p, 4)])
                rsqrt(st4[:, 4:5], st4[:, 4:5], 1.0, 3, [("st4", p, 4)], [("st4", p, 4)], "ln")
                yield
                ts("dve", zb[:, :], vg[:, :], st4[:, 3:4], st4[:, 4:5], ALU.subtract, ALU.mult,
                   [("vg", p, 0), ("vg", p, 1), ("st4", p, 3), ("st4", p, 4)], [("zb", p)])
                yield
                for g in range(8):
                    pz = ps[4 * p + 2 + g // 4]
                    kz = ("ps", 4 * p + 2 + g // 4)
                    mm(pz[:, (g % 4) * 128:(g % 4 + 1) * 128], zb[:, g * 128:(g + 1) * 128], wsT[:, g, :], True, True, [("zb", p), ("wsT", g)], [kz])
                yield
                for g in range(8):
                    pz = ps[4 * p + 2 + g // 4]
                    kz = ("ps", 4 * p + 2 + g // 4)
                    stt(tmpct[:, (g % 4) * 128:(g % 4 + 1) * 128], pz[:, (g % 4) * 128:(g % 4 + 1) * 128], pc("lnw", g), Cg[:, g, :], ALU.mult, ALU.add,
                        [kz, "pcol", ("Cg", g)], [("tmpct", p)])
                    tt("dve", uT[:, g, n * 128:(n + 1) * 128], uT[:, g, n * 128:(n + 1) * 128], tmpct[:, (g % 4) * 128:(g % 4 + 1) * 128], ALU.mult,
                       [("uT", g, n), ("tmpct", p)], [("uT", g, n)])
                    if g % 4 == 3:
                        yield
                pA, pB = pv
                for cc in range(8):
                    mm(pA[:, :], uT[:, cc, n * 128:(n + 1) * 128], Wout[:, cc, 0:512], cc == 0, cc == 7, [("uT", cc, n)] + Woutk, [kv[0]])
                for cc in range(8):
                    mm(pB[:, :], uT[:, cc, n * 128:(n + 1) * 128], Wout[:, cc, 512:1024], cc == 0, cc == 7, [("uT", cc, n)] + Woutk, [kv[1]])
                yield
                post_tail(pA, pB, kv[0], kv[1], src, dst, n, xt4[p][:, 0, :], otmp_[p], growb_, ssq_, rstd_, p=p)
                yield

            pend = [sgu_tile(n) for n in range(16)]
            active = []
            while active or pend:
                if pend and len(active) < 2:
                    active.append(pend.pop(0))
                for g_ in list(active):
                    try:
                        next(g_)
                    except StopIteration:
                        active.remove(g_)
            sched.flush(rec, final=False)

    ffn_phase(0, x1s, out_d if stage == 3 else x2s, "g02", 1, stage == 3)
    if stage == 3:
        return nc, st
    sgu_phase(x2s, out_d if stage == 4 else x1s)
    if stage == 4:
        sched.flush(rec, final=True)
        return nc, st
    ffn_phase(1, x1s, out_d, "g12", 3, True)
    return nc, st


def _col(v):
    return np.ascontiguousarray(np.asarray(v, np.float32).reshape(8, 128).T)


def _consts():
    p = np.arange(128)[:, None]
    f = np.arange(128)[None, :]
    ident = (p == f).astype(np.float32)
    mSL = np.tile((p < f).astype(np.float32), (1, 4))
    mLE = np.tile((p <= f).astype(np.float32), (1, 4))
    mGT = np.tile((p > f).astype(np.float32), (1, 4))
    keep = np.ones((128, 256), np.float32)
    keep[:, 0::128] = 0.0
    bones = ((p // 64) == (f // 64)).astype(np.float32)
    sel = ident.copy()
    ifold = ((p % 64) == np.arange(64)[None, :]).astype(np.float32)
    return np.concatenate([ident, mSL, mLE, mGT, keep, bones, sel, ifold], axis=1).astype(np.float32)


_CACHE = {}


def kernel(**inp):
    inp = {k: np.asarray(v) for k, v in inp.items()}
    x = inp["x"].astype(np.float32)
    ng = inp["norm_gains"]
    pcol = np.zeros((128, NPC), np.float32)
    for c in range(6):
        pcol[:, PC["mix"] + c * 8:PC["mix"] + c * 8 + 8] = _col(inp["rwkv_mix"][0, c])
    for nm, v in [("g00", ng[0, 0]), ("w0", inp["rwkv_w0"][0]), ("a0", inp["rwkv_a0"][0]), ("kk", inp["rwkv_k_k"][0]),
                  ("ka", inp["rwkv_k_a"][0]), ("rk", inp["rwkv_r_k"][0].reshape(-1)), ("gnw", inp["rwkv_gn_w"][0]),
                  ("gnb", inp["rwkv_gn_b"][0]), ("g02", ng[0, 2]), ("g10", ng[1, 0]), ("g12", ng[1, 2]),
                  ("binu", inp["sgu_b_in"][0, :1024]), ("lnw", inp["sgu_ln_w"][0]), ("lnb", inp["sgu_ln_b"][0])]:
        pcol[:, PC[nm]:PC[nm] + 8] = _col(v)
    pcol[:64, PC["hm"] + 0] = 1.0
    pcol[64:, PC["hm"] + 1] = 1.0
    pcol[:64, PC["hm"] + 2] = -1.0
    pcol[64:, PC["hm"] + 3] = -1.0
    brow = np.stack([np.broadcast_to(ng[i, j][None, :], (128, D)) for (i, j) in [(0, 1), (0, 3), (1, 1), (1, 3)]]).astype(np.float32)
    rowv = np.concatenate([inp["sgu_b_in"][0, 1024:], inp["sgu_bs"][0].reshape(-1), np.ones(128, np.float32)])[None, :].astype(np.float32)
    common = dict(
        pcol=pcol, brow=np.ascontiguousarray(brow), cst=_consts(), rowv=rowv,
        w_rkv=inp["rwkv_w_rkv"][0], w_o=inp["rwkv_w_o"][0], w1=inp["rwkv_w1"][0], w2=inp["rwkv_w2"][0],
        a1=inp["rwkv_a1"][0], a2=inp["rwkv_a2"][0], g1=inp["rwkv_g1"][0], g2=inp["rwkv_g2"][0],
        sgu_in=inp["sgu_w_in"][0], sgu_ws=inp["sgu_ws"][0], sgu_out=inp["sgu_w_out"][0],
        ffg=inp["ffn_w_gate"], ffu=inp["ffn_w_up"], ffd=inp["ffn_w_down"],
    )
    common = {k: np.ascontiguousarray(v, dtype=np.float32) for k, v in common.items()}
    in_maps = []
    for c in range(8):
        b, hf = c // 2, c % 2
        m = dict(common)
        m["xo"] = np.ascontiguousarray(x[b, hf * NTOK:(hf + 1) * NTOK])
        m["xp"] = np.ascontiguousarray(x[b, 0:NTOK]) if hf == 1 else np.zeros((NTOK, D), np.float32)
        in_maps.append(m)
    if "nc" not in _CACHE:
        _CACHE["nc"] = build_program()
    nc, _st = _CACHE["nc"]
    res = run_bass_kernel_spmd(nc, in_maps, core_ids=list(range(8)))
    out = np.zeros((4, 2 * NTOK, D), np.float32)
    for c in range(8):
        b, hf = c // 2, c % 2
        out[b, hf * NTOK:(hf + 1) * NTOK] = res.results[c]["out"]
    return out
```

```python
import numpy as np
from contextlib import ExitStack
import concourse.bass as bass
import concourse.mybir as mybir
from concourse.bass_utils import run_bass_kernel_spmd

F32 = mybir.dt.float32
BF16 = mybir.dt.bfloat16
AF = mybir.ActivationFunctionType
ALU = mybir.AluOpType

D = 1024
NTOK = 2048
DFF = 2816
NF = DFF // 128
CDEC = float(np.exp(-0.5))
DEBUG = {"stage": 99, "blocks": list(range(16))}


class Rec:
    def __init__(self):
        self.ops = []

    def add(self, eng, fn, r=(), w=(), dma=False):
        self.ops.append(dict(eng=eng, fn=fn, r=tuple(r), w=tuple(w), dma=dma))


class Sched:
    ENGS = ["pe", "dve", "act", "pool", "sp"]
    NDS = 24

    def __init__(self, nc, st):
        self.nc = nc
        self.esem = {e: st.enter_context(nc.semaphore("s_" + e)) for e in self.ENGS}
        self.dsem = [st.enter_context(nc.semaphore("d_%d" % k)) for k in range(self.NDS)]
        self.ecnt = {e: 0 for e in self.ENGS}
        self.dcnt = [0] * self.NDS
        self.nd = {"sp": 0, "pool": 0}

    def flush(self, rec, final=False):
        ops = rec.ops
        rec.ops = []
        n = len(ops)
        engs, NDS = self.ENGS, self.NDS
        pos = [0] * n
        cnt = {e: 0 for e in engs}
        last_of = {}
        for i, o in enumerate(ops):
            pos[i] = cnt[o["eng"]]
            cnt[o["eng"]] += 1
            last_of[o["eng"]] = i
        base_e = dict(self.ecnt)
        base_d = list(self.dcnt)
        last_w, readers = {}, {}
        ps_readers = {}
        deps = [dict() for _ in range(n)]
        dma_prev = [None] * NDS
        dma_sem_of, dma_val_of = {}, {}
        for i, o in enumerate(ops):
            dd = deps[i]
            for k in o["r"]:
                j = last_w.get(k)
                if j is not None:
                    dd[j] = True
            for k in o["w"]:
                j = last_w.get(k)
                if j is not None and j not in dd:
                    dd[j] = False
                for j in readers.get(k, ()):
                    if j not in dd:
                        dd[j] = False
            for k in o["r"]:
                if isinstance(k, tuple) and k[0] == "ps":
                    for e2, j in ps_readers.get(k, {}).items():
                        if e2 != o["eng"] and j not in dd:
                            dd[j] = False
            dd.pop(i, None)
            for k in o["r"]:
                readers.setdefault(k, []).append(i)
                if isinstance(k, tuple) and k[0] == "ps":
                    ps_readers.setdefault(k, {})[o["eng"]] = i
            for k in o["w"]:
                last_w[k] = i
                readers[k] = []
                if isinstance(k, tuple) and k[0] == "ps":
                    ps_readers[k] = {}
            if o["dma"]:
                half = NDS // 2
                s = (self.nd[o["eng"]] % half) + (0 if o["eng"] == "sp" else half)
                self.nd[o["eng"]] += 1
                if dma_prev[s] is not None:
                    dd[dma_prev[s]] = True
                dma_prev[s] = i
                self.dcnt[s] += 1
                dma_sem_of[i] = s
                dma_val_of[i] = 16 * self.dcnt[s]
        need_inc = [False] * n
        for i, o in enumerate(ops):
            keep = {}
            for j, raw in deps[i].items():
                p = ops[j]
                if p["dma"]:
                    keep[j] = raw
                    continue
                if p["eng"] == o["eng"]:
                    if o["dma"]:
                        keep[j] = raw
                    elif o["eng"] == "pe":
                        continue
                    elif raw or not DEBUG.get("relax", True):
                        keep[j] = raw
                    continue
                keep[j] = raw
            deps[i] = keep
            for j in keep:
                if not ops[j]["dma"]:
                    need_inc[j] = True
        for e, i in last_of.items():
            if not ops[i]["dma"]:
                need_inc[i] = True
        ordinal = [0] * n
        for i, o in enumerate(ops):
            if need_inc[i]:
                self.ecnt[o["eng"]] += 1
                ordinal[i] = self.ecnt[o["eng"]]
        waited = {e: {} for e in engs}
        waits = [None] * n
        for i, o in enumerate(ops):
            wl = {}
            for j in deps[i]:
                p = ops[j]
                if p["dma"]:
                    key, val = ("d", dma_sem_of[j]), dma_val_of[j]
                else:
                    key, val = ("e", p["eng"]), ordinal[j]
                if wl.get(key, 0) < val:
                    wl[key] = val
            out = []
            wd = waited[o["eng"]]
            for key, val in wl.items():
                if wd.get(key, 0) >= val:
                    continue
                wd[key] = val
                out.append((key, val))
            waits[i] = out
        esem, dsem = self.esem, self.dsem
        end_d = list(self.dcnt)

        def run_engine(ename, eobj):
            for e2 in engs:
                if e2 != ename and base_e[e2] > 0:
                    eobj.wait_ge(esem[e2], base_e[e2])
            for s in range(NDS):
                if base_d[s] > 0:
                    eobj.wait_ge(dsem[s], 16 * base_d[s])
            for i, o in enumerate(ops):
                if o["eng"] != ename:
                    continue
                for key, val in waits[i]:
                    sem = dsem[key[1]] if key[0] == "d" else esem[key[1]]
                    eobj.wait_ge(sem, val)
                ins = o["fn"](eobj)
                if o["dma"]:
                    ins.then_inc(dsem[dma_sem_of[i]], 16)
                elif need_inc[i]:
                    ins.then_inc(esem[ename], 1)
            if final and ename == "sp":
                for s in range(NDS):
                    if end_d[s] > 0:
                        eobj.wait_ge(dsem[s], 16 * end_d[s])

        with self.nc.Block() as block:
            @block.sync
            def _(e):
                run_engine("sp", e)

            @block.gpsimd
            def _(e):
                run_engine("pool", e)

            @block.tensor
            def _(e):
                run_engine("pe", e)

            @block.vector
            def _(e):
                run_engine("dve", e)

            @block.scalar
            def _(e):
                run_engine("act", e)


PC = dict(mix=0, g00=48, w0=56, a0=64, kk=72, ka=80, rk=88, gnw=96, gnb=104,
          g02=112, g10=120, g12=128, binu=136, lnw=144, lnb=152, hm=160)
NPC = 164


def build_program():
    nc = bass.Bass("TRN2", target_bir_lowering=False)
    st = ExitStack()
    rec = Rec()
    sched = Sched(nc, st)

    def din(name, shape):
        return nc.dram_tensor(name, list(shape), F32, kind="ExternalInput").ap()

    xo = din("xo", [NTOK, D])
    xp = din("xp", [NTOK, D])
    pcol_d = din("pcol", [128, NPC])
    brow_d = din("brow", [4, 128, D])
    cst_d = din("cst", [128, 128 + 3 * 512 + 256 + 128 + 128 + 64])
    rowv_d = din("rowv", [1, 2048 + 128])
    w_rkv_d = din("w_rkv", [3, D, D])
    w_o_d = din("w_o", [D, D])
    w1_d = din("w1", [D, 64])
    w2_d = din("w2", [64, D])
    a1_d = din("a1", [D, 64])
    a2_d = din("a2", [64, D])
    g1_d = din("g1", [D, 160])
    g2_d = din("g2", [160, D])
    sgu_in_d = din("sgu_in", [D, 2 * D])
    sgu_ws_d = din("sgu_ws", [8, 128, 128])
    sgu_out_d = din("sgu_out", [D, D])
    ffg_d = din("ffg", [2, D, DFF])
    ffu_d = din("ffu", [2, D, DFF])
    ffd_d = din("ffd", [2, DFF, D])
    out_d = nc.dram_tensor("out", [NTOK, D], F32, kind="ExternalOutput").ap()
    x1s = nc.dram_tensor("x1s", [NTOK, D], F32, kind="Internal").ap()

    def sb(name, shape, dt=F32):
        return st.enter_context(nc.sbuf_tensor("sb_" + name, list(shape), dt))

    ps = [st.enter_context(nc.psum_tensor("ps%d" % i, [128, 512], F32)) for i in range(8)]

    def mm(out, lhsT, rhs, start, stop, r, w):
        rec.add("pe", lambda e: e.matmul(out, lhsT, rhs, start=start, stop=stop), r, w)

    def act(out, in_, func, r, w, bias=None, scale=None, accum=None):
        kw = {}
        if bias is not None:
            kw["bias"] = bias
        if scale is not None:
            kw["scale"] = scale
        if accum is not None:
            kw["accum_out"] = accum
        rec.add("act", lambda e: e.activation(out, in_, func, **kw), r, w)

    def tt(eng, out, in0, in1, op, r, w):
        rec.add(eng, lambda e: e.tensor_tensor(out, in0, in1, op), r, w)

    def ts(eng, out, in0, s1, s2, op0, op1, r, w):
        if s2 is None:
            rec.add(eng, lambda e: e.tensor_scalar(out, in0, s1, None, op0), r, w)
        else:
            rec.add(eng, lambda e: e.tensor_scalar(out, in0, s1, s2, op0, op1), r, w)

    def stt(out, in0, scalar, in1, op0, op1, r, w):
        rec.add("dve", lambda e: e.scalar_tensor_tensor(out, in0, scalar, in1, op0, op1), r, w)

    def cp(eng, out, in_, r, w):
        rec.add(eng, lambda e: e.tensor_copy(out, in_), r, w)

    def dma(eng, out, in_, r, w):
        rec.add(eng, lambda e: e.dma_start(out=out, in_=in_), r, w, dma=True)

    pcol = sb("pcol", [128, NPC])
    cst = sb("cstf", [128, 256 + 128 + 64])
    identb = sb("identb", [128, 128], BF16)
    maskb = sb("maskb", [128, 3, 512], BF16)
    bonesb = sb("bonesb", [128, 128], BF16)
    selb = sb("selb", [128, 128], BF16)
    ifoldb = sb("ifoldb", [128, 64], BF16)
    dma("sp", pcol[:, :], pcol_d[:, :], [], ["pcol"])
    o_id, o_m, o_keep, o_bo, o_sel, o_if = 0, 128, 128 + 1536, 128 + 1536 + 256, 128 + 1536 + 384, 128 + 1536 + 512
    dma("sp", cst[:, 0:256], cst_d[:, o_keep:o_keep + 256], [], ["cst"])
    dma("sp", cst[:, 256:448], cst_d[:, o_sel:o_sel + 192], [], ["cst"])
    dma("pool", identb[:, :], cst_d[:, o_id:o_id + 128], [], ["identb"])
    dma("pool", maskb[:, :, :], cst_d[:, o_m:o_m + 1536].rearrange("p (a b) -> p a b", a=3), [], ["maskb"])
    dma("pool", bonesb[:, :], cst_d[:, o_bo:o_bo + 128], [], ["bonesb"])
    dma("pool", selb[:, :], cst_d[:, o_sel:o_sel + 128], [], ["selb"])
    dma("pool", ifoldb[:, :], cst_d[:, o_if:o_if + 64], [], ["ifoldb"])
    keepm = cst[:, 0:256]
    self32 = cst[:, 256:384]
    ifold = cst[:, 384:448]
    mSL, mLE, mGT = maskb[:, 0, :], maskb[:, 1, :], maskb[:, 2, :]

    def pc(name, k):
        c = PC[name] + k
        return pcol[:, c:c + 1]

    evac_flip = [0]

    def evac_copy(out, in_, r, w):
        evac_flip[0] ^= 1
        if evac_flip[0]:
            act(out, in_, AF.Copy, r, w)
        else:
            cp("dve", out, in_, r, w)

    epsT = sb("epsT", [128, 5])
    for ci, v in enumerate([1e-6, 1e-24, 64e-5, 1e-5, 1.0]):
        rec.add("pool", lambda e, ci=ci, v=v: e.memset(epsT[:, ci:ci + 1], v), [], ["epsT"])

    negc = sb("negc", [128, 16])
    rec.add("dve", lambda e: e.tensor_scalar(negc[:, 0:8], pcol[:, PC["w0"]:PC["w0"] + 8], -1.0, None, ALU.mult), ["pcol"], ["negc"])
    rec.add("dve", lambda e: e.tensor_scalar(negc[:, 8:16], pcol[:, PC["a0"]:PC["a0"] + 8], -1.0, None, ALU.mult), ["pcol", "negc"], ["negc"])

    def sigmoid_el(out, in_, negbias, r, w):
        act(out, in_, AF.Exp, r + ["negc"], w, scale=-1.0, bias=negbias)
        act(out, out, AF.Ln, w + ["epsT"], w, bias=epsT[:, 4:5])
        act(out, out, AF.Exp, w, w, scale=-1.0)

    def rsqrt(out, in_, scale, epscol, r, w, tag):
        act(out, in_, AF.Ln, r + ["epsT"], w, scale=scale, bias=epsT[:, epscol:epscol + 1])
        act(out, out, AF.Exp, w, w, scale=-0.5)

    junk = sb("junk", [128, 1024], BF16)

    def rms_stats(src_aps, ssq, rstd, ncols, rkeys, tag):
        for n, a in enumerate(src_aps):
            act(junk[:, :], a, AF.Square, rkeys[n] + ["cst"], ["junk", (tag, "ssq", n)], accum=ssq[:, n:n + 1])
        rsqrt(rstd[:, 0:ncols], ssq[:, 0:ncols], 1.0 / D, 0, [(tag, "ssq", n) for n in range(ncols)], [(tag, "rstd")], tag)

    stage = DEBUG["stage"]

    if DEBUG.get("skip_rwkv"):
        with ExitStack() as sa0:
            tmpx = sa0.enter_context(nc.sbuf_tensor("sa0_tmpx", [128, 4, D], F32))
            for g in range(4):
                dma("sp", tmpx[:, :, :], xo.rearrange("(n p) d -> p n d", p=128)[:, g * 4:(g + 1) * 4, :], [], ["tmpx"])
                dma("sp", x1s.rearrange("(n p) d -> p n d", p=128)[:, g * 4:(g + 1) * 4, :], tmpx[:, :, :], ["tmpx"], [("x1s", g)])
            sched.flush(rec, final=False)
    with ExitStack() as sa:
        if DEBUG.get("skip_rwkv"):
            DEBUG["blocks"] = []
        def sba(name, shape, dt=F32):
            return sa.enter_context(nc.sbuf_tensor("sa_" + name, list(shape), dt))

        Wr = sba("Wrkv", [128, 3, 8, D], BF16)
        Wo = sba("Wo", [128, 8, D], BF16)
        w1s = sba("w1s", [128, 8, 64], BF16)
        a1s = sba("a1s", [128, 8, 64], BF16)
        g1s = sba("g1s", [128, 8, 160], BF16)
        w2s = sba("w2s", [64, D], BF16)
        a2s = sba("a2s", [64, D], BF16)
        g2a = sba("g2a", [128, D], BF16)
        g2b = sba("g2b", [32, D], BF16)
        for c in range(3):
            for hf in range(2):
                dma("pool", Wr[:, c, hf * 4:(hf + 1) * 4, :],
                    w_rkv_d[c].rearrange("(k p) e -> p k e", p=128)[:, hf * 4:(hf + 1) * 4, :], [], [("Wr", c, hf)])
        dma("pool", w1s[:, :, :], w1_d.rearrange("(k p) e -> p k e", p=128), [], ["w1s"])
        dma("pool", a1s[:, :, :], a1_d.rearrange("(k p) e -> p k e", p=128), [], ["a1s"])
        dma("pool", g1s[:, :, :], g1_d.rearrange("(k p) e -> p k e", p=128), [], ["g1s"])
        dma("pool", w2s[:, :], w2_d[:, :], [], ["w2s"])
        dma("pool", a2s[:, :], a2_d[:, :], [], ["a2s"])
        dma("pool", g2a[:, :], g2_d[0:128, :], [], ["g2a"])
        dma("pool", g2b[:, :], g2_d[128:160, :], [], ["g2b"])
        for hf in range(2):
            dma("pool", Wo[:, hf * 4:(hf + 1) * 4, :],
                w_o_d.rearrange("(k p) e -> p k e", p=128)[:, hf * 4:(hf + 1) * 4, :], [], [("Wo", hf)])
        Wrk = lambda c: [("Wr", c, 0), ("Wr", c, 1)]
        Wok = [("Wo", 0), ("Wo", 1)]

        growb = sba("growb", [128, D])
        dma("sp", growb[:, :], brow_d[0], [], ["growb"])
        xin = sba("xin", [128, 2, D])
        xs_tm = sba("xs_tm", [128, 2, D], BF16)
        xnT = sba("xnT", [128, 8, 257], BF16)
        xx = sba("xx", [128, 8, 256], BF16)
        xsR = sba("xsR", [128, 8, 256], BF16)
        xsK = sba("xsK", [128, 8, 256], BF16)
        xsT = sba("xsT", [128, 8, 256], BF16)
        v_tm = sba("v_tm", [128, 2, D], BF16)
        v_fmp = [sba("v_fm%d" % p, [128, 256], BF16) for p in range(2)]
        hid_w = sba("hid_w", [64, 256], BF16)
        hid_a = sba("hid_a", [64, 256], BF16)
        hid_g = sba("hid_g", [128, 256], BF16)
        hid_g2 = sba("hid_g2", [32, 256], BF16)
        y_tmp = [sba("y_tm%d" % p, [128, 256]) for p in range(2)]
        ysqp = [sba("ysq%d" % p, [128, 256]) for p in range(2)]
        ssq = sba("ssq", [128, 4])
        rstd = sba("rstd", [128, 4])
        NT_ = 12
        T = [sba("T%d" % i, [128, 256]) for i in range(NT_)]
        csC = sba("csC", [128, 2])
        WC = sba("WC", [128, 2])
        DWp = [sba("DW%d" % p, [128, 2, 64]) for p in range(3)]
        fm = {nm: [sba("%s%d" % (nm, p), [128, 256], BF16) for p in range(3)]
              for nm in ("rT0", "rT1", "aT0", "aT1", "bT", "kT", "bh", "kh")}
        rkTp = [sba("rkT%d" % p, [128, 256], BF16) for p in range(3)]
        kk2 = sba("kk2", [128, 256], BF16)
        Bh_tm = [sba("Bh_tm%d" % p, [128, 256], BF16) for p in range(3)]
        Kh_tm = [sba("Kh_tm%d" % p, [128, 256], BF16) for p in range(3)]
        ZT = [[sba("ZT%d%d" % (p, i), [128, 512], BF16) for i in range(2)] for p in range(2)]
        SS = [[sba("SS%d%d" % (p, i), [128, 512], BF16) for i in range(2)] for p in range(2)]
        ArbT = [sba("ArbT%d" % p, [128, 512], BF16) for p in range(2)]
        AakT = [sba("AakT%d" % p, [128, 512], BF16) for p in range(2)]
        ArkT = [sba("ArkT%d" % p, [128, 512], BF16) for p in range(2)]
        Xb = [sba("Xb%d" % p, [128, 512], BF16) for p in range(2)]
        RpT = [sba("RpT%d" % p, [64, 512], BF16) for p in range(2)]
        Qh = [sba("Qh%d" % p, [64, 256]) for p in range(2)]
        PT = [sba("PT%d" % p, [64, 256]) for p in range(2)]
        Hb = [sba("Hb%d" % p, [64, 4, 64], BF16) for p in range(2)]
        Hst = sba("Hst", [64, 16, 64])
        ynbp = [sba("ynb%d" % p, [128, 256], BF16) for p in range(2)]
        gstp = [sba("gst%d" % p, [128, 2, 2, 4]) for p in range(2)]
        ygT = sba("ygT", [128, 8, 256], BF16)
        otmp = sba("otmp", [128, D])

        rec.add("pool", lambda e: e.memset(Hst[:, :, :], 0.0), [], [("Hst", h) for h in range(16)])
        rec.add("pool", lambda e: e.memset(xnT[:, :, 0:1], 0.0), [], [("xnT", dc) for dc in range(8)])

        class _Cut(Exception):
            pass

        def cut(k, buf_ap, keys, width=D):
            if DEBUG.get("cut") == k:
                dma("sp", out_d.rearrange("(n p) d -> p n d", p=128)[:, 0:2, 0:width], buf_ap, keys, ["cutout"])
                raise _Cut()

        TT0p = [sba("TT0%d" % p, [128, 256]) for p in range(2)]
        xres = sba("xres", [128, D])

        def pre_a(bi):
            pf = bi < 8
            tb = bi % 8
            cl = ["k", "v", "w", "a"] if pf else ["k", "v", "w", "a", "r", "g"]
            cidx = dict(r=0, k=1, v=2, w=3, a=4, g=5)
            src = (xp if pf else xo).rearrange("(n p) d -> p n d", p=128)[:, tb * 2:tb * 2 + 2, :]
            dma("sp", xin[:, :, :], src, [], [("xin", 0), ("xin", 1)])
            cut(1, xin[:, :, :], [("xin", 0), ("xin", 1)])
            rms_stats([xin[:, n, :] for n in range(2)], ssq, rstd, 2, [[("xin", n)] for n in range(2)], "nA")
            for n in range(2):
                act(xs_tm[:, n, :], xin[:, n, :], AF.Copy, [("xin", n), ("nA", "rstd")], [("xs_tm", n)], scale=rstd[:, n:n + 1])
            cut(2, xs_tm[:, :, :], [("xs_tm", 0), ("xs_tm", 1)])
            yield
            for dc in range(8):
                pb = ps[6 + dc % 2]
                for n in range(2):
                    mm(pb[:, n * 128:(n + 1) * 128], xs_tm[:, n, dc * 128:(dc + 1) * 128], identb[:, :], True, True,
                       [("xs_tm", n), "identb"], [("ps", 6 + dc % 2)])
                if dc % 2:
                    act(xnT[:, dc, 1:257], pb[:, 0:256], AF.Copy, [("ps", 6 + dc % 2), "pcol"], [("xnT", dc)],
                        scale=pc("g00", dc))
                else:
                    ts("dve", xnT[:, dc, 1:257], pb[:, 0:256], pc("g00", dc), None, ALU.mult, None,
                       [("ps", 6 + dc % 2), "pcol"], [("xnT", dc)])
            yield

        prefetched = set()

        def blk_pre(bi):
            pf = bi < 8
            tb = bi % 8
            cl = ["k", "v", "w", "a"] if pf else ["k", "v", "w", "a", "r", "g"]
            cidx = dict(r=0, k=1, v=2, w=3, a=4, g=5)
            if bi not in prefetched:
                yield from pre_a(bi)
            for dc in range(8):
                tt("dve", xx[:, dc, :], xnT[:, dc, 0:256], xnT[:, dc, 1:257], ALU.subtract, [("xnT", dc)], [("xx", dc)])

            def mix_into(buf, ci, keyf):
                for dc in range(8):
                    stt(buf[:, dc, :], xx[:, dc, :], pc("mix", ci * 8 + dc), xnT[:, dc, 1:257], ALU.mult, ALU.add,
                        [("xx", dc), ("xnT", dc), "pcol"], [keyf(dc)])

            mix_into(xsK, 1, lambda dc: ("xs", 1, dc))
            if not pf:
                mix_into(xsR, 0, lambda dc: ("xs", 0, dc))
            tk = lambda dc: ("xsT", dc)
            yield
            mix_into(xsT, 3, tk)
            pb = ps[6]
            for dc in range(8):
                mm(pb[0:64, 0:256], w1s[:, dc, :], xsT[:, dc, :], dc == 0, dc == 7, ["w1s", tk(dc)], [("ps", 6)])
            act(hid_w[:, :], pb[0:64, 0:256], AF.Tanh, [("ps", 6)], ["hid_w"])
            mix_into(xsT, 4, tk)
            pb = ps[7]
            for dc in range(8):
                mm(pb[0:64, 0:256], a1s[:, dc, :], xsT[:, dc, :], dc == 0, dc == 7, ["a1s", tk(dc)], [("ps", 7)])
            cp("dve", hid_a[:, :], pb[0:64, 0:256], [("ps", 7)], ["hid_a"])
            if not pf:
                mix_into(xsT, 5, tk)
                pb = ps[6]
                for dc in range(8):
                    mm(pb[:, 0:256], g1s[:, dc, 0:128], xsT[:, dc, :], dc == 0, dc == 7, ["g1s", tk(dc)], [("ps", 6)])
                for dc in range(8):
                    mm(pb[0:32, 256:512], g1s[:, dc, 128:160], xsT[:, dc, :], dc == 0, dc == 7, ["g1s", tk(dc)], [("ps", 6)])
                act(hid_g[:, :], pb[:, 0:256], AF.Sigmoid, [("ps", 6)], ["hid_g"])
                act(hid_g2[:, :], pb[0:32, 256:512], AF.Sigmoid, [("ps", 6)], ["hid_g2"])
            yield
            mix_into(xsT, 2, tk)
            for dc in range(8):
                cp("pool", xnT[:, dc, 0:1], xnT[:, dc, 256:257], [("xnT", dc), ("xx", dc), tk(dc), ("xs", 1, dc), ("xs", 0, dc)],
                   [("xnT", dc)])
            for n in range(2):
                for eh in range(2):
                    pb = ps[6 + eh]
                    for dc in range(8):
                        mm(pb[:, :], xsT[:, dc, n * 128:(n + 1) * 128], Wr[:, 2, dc, eh * 512:(eh + 1) * 512], dc == 0, dc == 7,
                           [("xsT", dc)] + Wrk(2), [("ps", 6 + eh)])
                    evac_copy(v_tm[:, n, eh * 512:(eh + 1) * 512], pb[:, :], [("ps", 6 + eh)], [("v_tm", n, eh)])
            vk = lambda n: [("v_tm", n, 0), ("v_tm", n, 1)]
            cut(4, xin[:, :, :], vk(0) + vk(1) + ["hid_w", "hid_a", "hid_g", "hid_g2"])

            yield

        def hp_work(bi, hp):
            pf = bi < 8
            tb = bi % 8
            cl = ["k", "v", "w", "a"] if pf else ["k", "v", "w", "a", "r", "g"]
            cidx = dict(r=0, k=1, v=2, w=3, a=4, g=5)
            vk = lambda n: [("v_tm", n, 0), ("v_tm", n, 1)]
            par = hp % 2
            pp = hp % 3
            DW, rkT, v_fm, y_tm, ysq, ynb, gst = DWp[pp], rkTp[pp], v_fmp[par], y_tmp[par], ysqp[par], ynbp[par], gstp[par]
            TT0, TT1 = TT0p[par], ysqp[par]
            hsl = slice(hp * 128, hp * 128 + 128)
            sgw, cs_, t2, E1, E2, E3, E4, asig, kkt, kkn, beta, kmod = T
            pb = ps[6]
            mm(pb[:, 0:256], w2s[:, hsl], hid_w[:, :], True, True, ["w2s", "hid_w"], [("ps", 6)])
            sigmoid_el(sgw[:, :], pb[:, 0:256], negc[:, hp:hp + 1], [("ps", 6)], ["sgw"])
            rec.add("dve", lambda e, a=cs_, b=sgw: e.tensor_tensor_scan(a[:, :], keepm, b[:, :], 0.0, ALU.mult, ALU.subtract),
                    ["sgw", "cst"], ["cs"])
            tt("dve", t2[:, :], cs_[:, :], sgw[:, :], ALU.add, ["cs", "sgw"], ["t2"])
            if not pf:
                act(E1[:, :], cs_[:, :], AF.Exp, ["cs"], ["E1"], scale=CDEC)
            act(E2[:, :], t2[:, :], AF.Exp, ["t2"], ["E2"], scale=CDEC)
            act(E3[:, :], cs_[:, :], AF.Exp, ["cs"], ["E3"], scale=-CDEC)
            ts("dve", csC[:, :], cs_[:, 127:256:128], CDEC, None, ALU.mult, None, ["cs"], ["csC"])
            act(WC[:, :], csC[:, :], AF.Exp, ["csC"], ["WC"])
            for c in range(2):
                act(E4[:, c * 128:(c + 1) * 128], cs_[:, c * 128:(c + 1) * 128], AF.Exp, ["cs", "csC"], [("E4", c)],
                    scale=-CDEC, bias=csC[:, c:c + 1])
                ts("pool", DW[:, c, :], ifold, WC[:, c:c + 1], None, ALU.mult, None, ["cst", "WC"], [("DW", pp, c)])
            cut(50, xin[:, :, :], ["sgw"])
            cut(51, xin[:, :, :], ["cs"])
            cut(52, xin[:, :, :], ["t2", "E1", "E2", "E3"])
            cut(53, xin[:, :, :], ["WC", ("E4", 0), ("E4", 1), ("DW", pp, 0), ("DW", pp, 1)])
            yield
            pb = ps[7]
            mm(pb[:, 0:256], a2s[:, hsl], hid_a[:, :], True, True, ["a2s", "hid_a"], [("ps", 7)])
            sigmoid_el(asig[:, :], pb[:, 0:256], negc[:, 8 + hp:9 + hp], [("ps", 7)], ["asig"])
            yield
            pk = ps[6]
            for dc in range(8):
                mm(pk[:, 0:256], Wr[:, 1, dc, hsl], xsK[:, dc, :], dc == 0, dc == 7, Wrk(1) + [("xs", 1, dc)], [("ps", 6)])
            ts("dve", kkt[:, :], pk[:, 0:256], pc("kk", hp), None, ALU.mult, None, [("ps", 6), "pcol"], ["kkt"])
            tt("dve", kk2[:, :], kkt[:, :], kkt[:, :], ALU.mult, ["kkt"], ["kk2"])
            pn = ps[7]
            mm(pn[:, 256:512], bonesb[:, :], kk2[:, :], True, True, ["bonesb", "kk2"], [("ps", 7)])
            rsqrt(kkn[:, :], pn[:, 256:512], 1.0, 1, [("ps", 7)], ["kkn"], "kkn")
            tt("dve", kkn[:, :], kkn[:, :], kkt[:, :], ALU.mult, ["kkn", "kkt"], ["kkn"])
            cut(54, xin[:, :, :], ["kkn", "asig"])
            tt("dve", beta[:, :], kkn[:, :], asig[:, :], ALU.mult, ["kkn", "asig"], ["beta"])
            for hh_ in range(2):
                stt(fm["aT%d" % hh_][pp][:, :], kkn[:, :], pcol[:, PC["hm"] + 2 + hh_:PC["hm"] + 3 + hh_], E2[:, :], ALU.mult, ALU.mult,
                    ["kkn", "E2", "pcol"], [("aT", pp, hh_)])
            tt("pool", fm["bT"][pp][:, :], beta[:, :], E3[:, :], ALU.mult, ["beta", "E3"], [("bT", pp)])
            tt("pool", fm["bh"][pp][:, :], beta[:, :], E4[:, :], ALU.mult, ["beta", ("E4", 0), ("E4", 1)], [("bh", pp)])
            yield
            ts("dve", t2[:, :], asig[:, :], pc("ka", hp), pc("ka", hp), ALU.mult, ALU.subtract, ["asig", "pcol"], ["t2"])
            stt(kmod[:, :], t2[:, :], 1.0, pk[:, 0:256], ALU.add, ALU.mult, ["t2", ("ps", 6)], ["kmod"])
            tt("pool", fm["kT"][pp][:, :], kmod[:, :], E3[:, :], ALU.mult, ["kmod", "E3"], [("kT", pp)])
            tt("pool", fm["kh"][pp][:, :], kmod[:, :], E4[:, :], ALU.mult, ["kmod", ("E4", 0), ("E4", 1)], [("kh", pp)])
            if not pf:
                pr = ps[7]
                for dc in range(8):
                    mm(pr[:, 0:256], Wr[:, 0, dc, hsl], xsR[:, dc, :], dc == 0, dc == 7, Wrk(0) + [("xs", 0, dc)], [("ps", 7)])
                stt(rkT[:, :], pr[:, 0:256], pc("rk", hp), kmod[:, :], ALU.mult, ALU.mult, [("ps", 7), "pcol", "kmod"], [("rkT", pp)])
                for hh_ in range(2):
                    stt(fm["rT%d" % hh_][pp][:, :], pr[:, 0:256], pcol[:, PC["hm"] + hh_:PC["hm"] + 1 + hh_], E1[:, :], ALU.mult, ALU.mult,
                        [("ps", 7), "E1", "pcol"], [("rT", pp, hh_)])
            cut(55, xin[:, :, :], [("rT", pp, 0), ("aT", pp, 0), ("aT", pp, 1), ("bT", pp), ("kT", pp), ("rkT", pp), ("bh", pp), ("kh", pp)])
            yield
            pb = ps[7]
            for c in range(2):
                mm(pb[:, c * 128:(c + 1) * 128], fm["bh"][pp][:, c * 128:(c + 1) * 128], identb[:, :], True, True,
                   [("bh", pp), "identb"], [("ps", 7)])
                mm(pb[:, 256 + c * 128:256 + (c + 1) * 128], fm["kh"][pp][:, c * 128:(c + 1) * 128], identb[:, :], True, True,
                   [("kh", pp), "identb"], [("ps", 7)])
            cut(56, xin[:, :, :], [("ps", 7)])
            if DEBUG.get("var") == "bh_dve":
                cp("dve", Bh_tm[pp][:, :], pb[:, 0:256], [("ps", 7)], [("Bh_tm", pp)])
            else:
                act(Bh_tm[pp][:, :], pb[:, 0:256], AF.Copy, [("ps", 7)], [("Bh_tm", pp)])
            cp("dve", Kh_tm[pp][:, :], pb[:, 256:512], [("ps", 7)], [("Kh_tm", pp)])

            cut(5, xin[:, :, :], [("Bh_tm", pp), ("Kh_tm", pp), ("rT", pp, 0), ("aT", pp, 0), ("aT", pp, 1), ("bT", pp), ("kT", pp), ("rkT", pp), ("DW", pp, 0), ("DW", pp, 1)])
            yield "prep_done"
            b0, b1, b2 = ps[par * 3], ps[par * 3 + 1], ps[par * 3 + 2]
            k0, k1, k2 = ("ps", par * 3), ("ps", par * 3 + 1), ("ps", par * 3 + 2)
            aTm = [fm["aT0"][pp], fm["aT1"][pp]]
            rTm = [fm["rT0"][pp], fm["rT1"][pp]]
            bT_, kT_ = fm["bT"][pp], fm["kT"][pp]
            slots = [(hh, c) for hh in range(2) for c in range(2)]

            def sl(q):
                return slice(q * 128, (q + 1) * 128)

            def csl(c):
                return slice(c * 128, (c + 1) * 128)

            for q, (hh, c) in enumerate(slots):
                mm(b0[:, sl(q)], bT_[:, csl(c)], aTm[hh][:, csl(c)], True, True, [("bT", pp), ("aT", pp, hh)], [k0])
                mm(b1[:, sl(q)], aTm[hh][:, csl(c)], bT_[:, csl(c)], True, True, [("bT", pp), ("aT", pp, hh)], [k1])
                mm(b2[:, sl(q)], kT_[:, csl(c)], aTm[hh][:, csl(c)], True, True, [("kT", pp), ("aT", pp, hh)], [k2])
            yield
            cut(61, xin[:, :, :], [k0, k1, k2])
            tt("dve", ZT[par][0][:, :], b0[:, :], mSL, ALU.mult, [k0, "maskb"], [("ZT", par, 0)])
            tt("dve", SS[par][0][:, :], b1[:, :], mGT, ALU.mult, [k1, "maskb"], [("SS", par, 0)])
            tt("dve", AakT[par][:, :], b2[:, :], mSL, ALU.mult, [k2, "maskb"], [("AakT", par)])
            if not pf:
                for q, (hh, c) in enumerate(slots):
                    mm(b0[:, sl(q)], bT_[:, csl(c)], rTm[hh][:, csl(c)], True, True, [("bT", pp), ("rT", pp, hh)], [k0])
                    mm(b1[:, sl(q)], kT_[:, csl(c)], rTm[hh][:, csl(c)], True, True, [("kT", pp), ("rT", pp, hh)], [k1])
                tt("dve", ArbT[par][:, :], b0[:, :], mLE, ALU.mult, [k0, "maskb"], [("ArbT", par)])
                tt("dve", ArkT[par][:, :], b1[:, :], mLE, ALU.mult, [k1, "maskb"], [("ArkT", par)])
            cut(62, xin[:, :, :], [("ZT", par, 0), ("SS", par, 0), ("AakT", par), ("ArbT", par), ("ArkT", par)])
            yield
            for q, (hh, c) in enumerate(slots):
                hc = slice((hp * 2 + hh) * 64, (hp * 2 + hh) * 64 + 64)
                mm(b2[:, q * 128:q * 128 + 64], AakT[par][:, sl(q)], v_tm[:, c, hc], True, True,
                   [("AakT", par)] + vk(c), [k2])
                mm(b2[:, q * 128 + 64:(q + 1) * 128], aTm[hh][:, csl(c)], ifoldb[:, :],
                   True, True, [("aT", pp, hh), "ifoldb"], [k2])
            cut(63, xin[:, :, :], [k2])
            evac_copy(Xb[par][:, :], b2[:, :], [k2], [("Xb", par)])
            cut(64, xin[:, :, :], [("Xb", par)])
            for k in range(7):
                zi, zo = k % 2, (k + 1) % 2
                if k < 5:
                    for q in range(4):
                        mm(b0[:, sl(q)], ZT[par][zi][:, sl(q)], SS[par][zi][:, sl(q)], True, True,
                           [("ZT", par, zi), ("SS", par, zi)], [k0])
                    for q in range(4):
                        mm(b1[:, sl(q)], SS[par][zi][:, sl(q)], ZT[par][zi][:, sl(q)], True, True,
                           [("ZT", par, zi), ("SS", par, zi)], [k1])
                for q in range(4):
                    mm(b2[:, sl(q)], ZT[par][zi][:, sl(q)], Xb[par][:, sl(q)], True, True,
                       [("ZT", par, zi), ("Xb", par)], [k2])
                if k < 5:
                    act(SS[par][zo][:, :], b0[:, :], AF.Copy, [k0], [("SS", par, zo)])
                if k < 6:
                    act(ZT[par][zo][:, :], b1[:, :], AF.Copy, [k1], [("ZT", par, zo)])
                tt("dve", Xb[par][:, :], b2[:, :], Xb[par][:, :], ALU.add, [k2, ("Xb", par)], [("Xb", par)])
                yield
            UA = Xb[par]
            cut(6, xin[:, :, :], [("Xb", par), ("ArbT", par), ("ArkT", par)])
            yield
            if not pf:
                for q, (hh, c) in enumerate(slots):
                    mm(b0[0:64, sl(q)], UA[:, q * 128 + 64:(q + 1) * 128], ArbT[par][:, sl(q)], True, False,
                       [("Xb", par), ("ArbT", par)], [k0])
                    mm(b0[0:64, sl(q)], ifoldb[:, :], rTm[hh][:, csl(c)], False, True,
                       ["ifoldb", ("rT", pp, hh)], [k0])
                evac_copy(RpT[par][:, :], b0[0:64, :], [k0], [("RpT", par)])
            for q, (hh, c) in enumerate(slots):
                hc = slice((hp * 2 + hh) * 64, (hp * 2 + hh) * 64 + 64)
                mm(b1[0:64, q * 64:(q + 1) * 64], Bh_tm[pp][:, c * 128 + hh * 64:c * 128 + (hh + 1) * 64], UA[:, q * 128:q * 128 + 64], True, False,
                   [("Bh_tm", pp), ("Xb", par)], [k1])
                mm(b1[0:64, q * 64:(q + 1) * 64], Kh_tm[pp][:, c * 128 + hh * 64:c * 128 + (hh + 1) * 64], v_tm[:, c, hc], False, True,
                   [("Kh_tm", pp)] + vk(c), [k1])
                mm(b1[0:64, 256 + q * 64:256 + (q + 1) * 64], UA[:, q * 128 + 64:(q + 1) * 128], Bh_tm[pp][:, c * 128 + hh * 64:c * 128 + (hh + 1) * 64],
                   True, False, [("Bh_tm", pp), ("Xb", par)], [k1])
                mm(b1[0:64, 256 + q * 64:256 + (q + 1) * 64], self32[:, hh * 64:(hh + 1) * 64], DW[:, c, :], False, True,
                   ["cst", ("DW", pp, c)], [k1])
            cp("dve", Qh[par][:, :], b1[0:64, 0:256], [k1], [("Qh", par)])
            act(PT[par][:, :], b1[0:64, 256:512], AF.Copy, [k1], [("PT", par)])
            cut(7, xin[:, :, :], [("Qh", par), ("PT", par), ("RpT", par)])
            yield
            for q, (hh, c) in enumerate(slots):
                hd = hp * 2 + hh
                if not pf:
                    act(Hb[par][:, q, :], Hst[:, hd, :], AF.Copy, [("Hst", hd)], [("Hb", par, q)])
                mm(b2[0:64, q * 64:(q + 1) * 64], PT[par][:, q * 64:(q + 1) * 64], Hst[:, hd, :], True, True, [("PT", par), ("Hst", hd)], [k2])
                tt("dve", Hst[:, hd, :], b2[0:64, q * 64:(q + 1) * 64], Qh[par][:, q * 64:(q + 1) * 64], ALU.add, [k2, ("Qh", par)], [("Hst", hd)])
                yield
            yield
            if not pf:
                for q, (hh, c) in enumerate(slots):
                    hc = slice((hp * 2 + hh) * 64, (hp * 2 + hh) * 64 + 64)
                    oc = slice(c * 128 + hh * 64, c * 128 + hh * 64 + 64)
                    mm(b0[:, oc], RpT[par][:, sl(q)], Hb[par][:, q, :], True, False, [("RpT", par), ("Hb", par, q)], [k0])
                    mm(b0[:, oc], ArbT[par][:, sl(q)], UA[:, q * 128:q * 128 + 64], False, False, [("ArbT", par), ("Xb", par)], [k0])
                    mm(b0[:, oc], ArkT[par][:, sl(q)], v_tm[:, c, hc], False, True, [("ArkT", par)] + vk(c), [k0])
                evac_copy(y_tm[:, :], b0[:, 0:256], [k0], [("y_tm", par)])
                if stage == 1:
                    dma("sp", out_d.rearrange("(n p) d -> p n d", p=128)[:, tb * 2:tb * 2 + 2, hsl], y_tm[:, :].rearrange("p (c k) -> p c k", c=2),
                        [("y_tm", par)], [("outblk", tb, hp)])
                    return
                yield
                yv = y_tm[:, :].rearrange("p (a k) -> p a k", k=64)
                g4 = gst[:, :, :, :].rearrange("p n h s -> p (n h) s")
                rec.add("dve", lambda e, o=g4[:, :, 0], i=yv: e.tensor_reduce(o, i, mybir.AxisListType.X, ALU.add),
                        [("y_tm", par)], [("gst", par, 0)])
                tt("dve", ysq[:, :], y_tm[:, :], y_tm[:, :], ALU.mult, [("y_tm", par)], [("ysq", par)])
                rec.add("dve", lambda e, o=g4[:, :, 1], i=ysq[:, :].rearrange("p (a k) -> p a k", k=64):
                        e.tensor_reduce(o, i, mybir.AxisListType.X, ALU.add), [("ysq", par)], [("gst", par, 1)])
                ts("dve", g4[:, :, 0], g4[:, :, 0], 1.0 / 64, None, ALU.mult, None, [("gst", par, 0)], [("gst", par, 0)])
                tt("dve", g4[:, :, 2], g4[:, :, 0], g4[:, :, 0], ALU.mult, [("gst", par, 0)], [("gst", par, 2)])
                stt(g4[:, :, 3], g4[:, :, 1], 1.0 / 64, g4[:, :, 2], ALU.mult, ALU.subtract,
                    [("gst", par, 1), ("gst", par, 2)], [("gst", par, 3)])
                rsqrt(g4[:, :, 3], g4[:, :, 3], 1.0, 2, [("gst", par, 3)], [("gst", par, 3)], "gst")
                tt("dve", ysq[:, :].rearrange("p (a k) -> p a k", k=64), yv,
                   g4[:, :, 0:1].to_broadcast([128, 4, 64]), ALU.subtract, [("y_tm", par), ("gst", par, 0), ("ysq", par)], [("ysq", par)])
                tt("dve", ynb[:, :].rearrange("p (a k) -> p a k", k=64), ysq[:, :].rearrange("p (a k) -> p a k", k=64),
                   g4[:, :, 3:4].to_broadcast([128, 4, 64]), ALU.mult, [("ysq", par), ("gst", par, 3)], [("ynb", par)])
                yield
                pa, pb2 = b1, b2
                for n in range(2):
                    mm(pa[:, n * 128:(n + 1) * 128], ynb[:, n * 128:(n + 1) * 128], identb[:, :], True, True, [("ynb", par), "identb"], [k1])
                    mm(pb2[:, 256 + n * 128:256 + (n + 1) * 128], v_tm[:, n, hsl], identb[:, :], True, True, vk(n) + ["identb"], [k2])
                mm(pa[:, 256:512], bonesb[:, :], rkT[:, :], True, True, ["bonesb", ("rkT", pp)], [k1])
                mm(pb2[:, 0:256], g2a[:, hsl], hid_g[:, :], True, False, ["g2a", "hid_g"], [k2])
                mm(pb2[:, 0:256], g2b[:, hsl], hid_g2[:, :], False, True, ["g2b", "hid_g2"], [k2])
                t0, t1 = TT0, TT1
                act(v_fm[:, :], pb2[:, 256:512], AF.Copy, [k2], [("v_fm", par)])
                ts("dve", t0[:, :], pa[:, 0:256], pc("gnw", hp), pc("gnb", hp), ALU.mult, ALU.add, [k1, "pcol"], [("tt0", par)])
                tt("dve", t1[:, :], pa[:, 256:512], v_fm[:, :], ALU.mult, [k1, ("v_fm", par)], [("ysq", par)])
                tt("dve", t0[:, :], t0[:, :], t1[:, :], ALU.add, [("tt0", par), ("ysq", par)], [("tt0", par)])
                tt("dve", ygT[:, hp, :], pb2[:, 0:256], t0[:, :], ALU.mult, [k2, ("tt0", par)], [("ygT", hp)])

            yield

        dstall = (out_d if stage == 2 else x1s).rearrange("(n p) d -> p n d", p=128)

        def blk_post(bi):
            pf = bi < 8
            tb = bi % 8
            cl = ["k", "v", "w", "a"] if pf else ["k", "v", "w", "a", "r", "g"]
            cidx = dict(r=0, k=1, v=2, w=3, a=4, g=5)
            vk = lambda n: [("v_tm", n, 0), ("v_tm", n, 1)]
            for n in range(2):
                for eh in range(2):
                    pb = ps[6 + eh]
                    for cc in range(8):
                        mm(pb[:, :], ygT[:, cc, n * 128:(n + 1) * 128], Wo[:, cc, eh * 512:(eh + 1) * 512], cc == 0, cc == 7,
                           [("ygT", cc)] + Wok, [("ps", 6 + eh)])
                for eh in range(2):
                    act(junk[:, 0:512], ps[6 + eh][:, :], AF.Square, [("ps", 6 + eh)], ["junk", ("pn", "ssq", eh)], accum=ssq[:, 2 + eh:3 + eh])
                tt("dve", rstd[:, 2:3], ssq[:, 2:3], ssq[:, 3:4], ALU.add, [("pn", "ssq", 0), ("pn", "ssq", 1)], [("pn", "rstd")])
                rsqrt(rstd[:, 2:3], rstd[:, 2:3], 1.0 / D, 0, [("pn", "rstd")], [("pn", "rstd")], "pn")
                for eh in range(2):
                    stt(otmp[:, eh * 512:(eh + 1) * 512], ps[6 + eh][:, :], rstd[:, 2:3], growb[:, eh * 512:(eh + 1) * 512], ALU.mult, ALU.mult,
                        [("ps", 6 + eh), ("pn", "rstd"), "growb"], ["otmp"])
                dma("sp", xres[:, :], xo.rearrange("(n p) d -> p n d", p=128)[:, tb * 2 + n, :], [], ["xres"])
                tt("dve", xres[:, :], xres[:, :], otmp[:, :], ALU.add, ["xres", "otmp"], ["xres"])
                dma("sp", dstall[:, tb * 2 + n, :], xres[:, :], ["xres"], [("x1s", tb, n)])


            yield

        def drive(gens, depth):
            active = []
            pend = list(gens)
            prev_ready = True
            while active or pend:
                if pend and len(active) < depth and prev_ready:
                    g_ = pend.pop(0)
                    active.append({"g": g_[0], "st": "prep", "hook": g_[1]} if isinstance(g_, tuple) else {"g": g_, "st": "prep"})
                    prev_ready = False
                for ent in list(active):
                    if ent["st"] == "wait":
                        if sum(1 for e in active if e["st"] == "scan") < 2:
                            ent["st"] = "scan"
                        else:
                            continue
                    try:
                        v = next(ent["g"])
                        if v == "prep_done":
                            ent["st"] = "wait"
                            prev_ready = True
                            if ent.get("hook"):
                                ent["hook"]()
                    except StopIteration:
                        if ent["st"] == "prep":
                            prev_ready = True
                        active.remove(ent)

        def run_all(g):
            for _ in g:
                pass

        try:
            for bi in range(16):
                if bi not in DEBUG["blocks"]:
                    continue
                run_all(blk_pre(bi))
                nxt = bi + 1

                def hook(nxt=nxt):
                    if nxt < 16 and nxt in DEBUG["blocks"] and DEBUG.get("prefetch", True):
                        run_all(pre_a(nxt))
                        prefetched.add(nxt)

                drive([(hp_work(bi, hp), hook) if hp == 5 else hp_work(bi, hp) for hp in range(8)], DEBUG.get("depth", 3))
                if bi >= 8 and stage != 1:
                    run_all(blk_post(bi))
        except _Cut:
            sched.flush(rec, final=True)
            return nc, st
        sched.flush(rec, final=(stage <= 2))
    if stage <= 2:
        return nc, st

    x2s = nc.dram_tensor("x2s", [NTOK, D], F32, kind="Internal").ap()

    def tiles(ap):
        return ap.rearrange("(n p) d -> p n d", p=128)

    def norm_to_fm(sp_, src, hT, gname, xt4, xs4, ssq16, rstd16):
        for g in range(8):
            xb_, sq_ = xt4[g % 2], xs4[g % 2]
            gk = g % 2
            dma("sp", xb_[:, :, :], tiles(src)[:, g * 2:(g + 1) * 2, :], [], [("xt4", gk, n) for n in range(2)])
            for n in range(2):
                act(junk[:, :], xb_[:, n, :], AF.Square, [("xt4", gk, n)], ["junk", ("ni", "ssq", gk)], accum=ssq16[:, gk * 2 + n:gk * 2 + n + 1])
            rsqrt(rstd16[:, gk * 2:gk * 2 + 2], ssq16[:, gk * 2:gk * 2 + 2], 1.0 / D, 0, [("ni", "ssq", gk)], [("ni", "rstd", gk)], "ni")
            for n in range(2):
                if n % 2:
                    act(sq_[:, n, :], xb_[:, n, :], AF.Copy, [("xt4", gk, n), ("ni", "rstd", gk)], [("xs4", gk, n)],
                        scale=rstd16[:, gk * 2 + n:gk * 2 + n + 1])
                else:
                    ts("dve", sq_[:, n, :], xb_[:, n, :], rstd16[:, gk * 2 + n:gk * 2 + n + 1], None, ALU.mult, None,
                       [("xt4", gk, n), ("ni", "rstd", gk)], [("xs4", gk, n)])
            for dc in range(8):
                pb = ps[6 + dc % 2]
                for n in range(2):
                    mm(pb[:, n * 128:(n + 1) * 128], sq_[:, n, dc * 128:(dc + 1) * 128], identb[:, :], True, True,
                       [("xs4", gk, n), "identb"], [("ps", 6 + dc % 2)])
                if dc % 2:
                    act(hT[:, dc, g * 256:(g + 1) * 256], pb[:, 0:256], AF.Copy, [("ps", 6 + dc % 2), "pcol"], [("hT", dc, g // 2)],
                        scale=pc(gname, dc))
                else:
                    ts("dve", hT[:, dc, g * 256:(g + 1) * 256], pb[:, 0:256], pc(gname, dc), None, ALU.mult, None,
                       [("ps", 6 + dc % 2), "pcol"], [("hT", dc, g // 2)])

    def post_tail(pA, pB, kA, kB, src, dst, n, xt, otmp_, growb_, ssq_, rstd_, p=0):
        c0 = 4 + 2 * p
        kx = ("xt4", p, 0)
        dma("sp", xt[:, :], tiles(src)[:, n, :], [], [kx])
        act(junk[:, 0:512], pA[:, :], AF.Square, [kA], ["junk", ("pt", p, "s0")], accum=ssq_[:, c0:c0 + 1])
        act(junk[:, 512:1024], pB[:, :], AF.Square, [kB], ["junk", ("pt", p, "s1")], accum=ssq_[:, c0 + 1:c0 + 2])
        tt("dve", rstd_[:, c0:c0 + 1], ssq_[:, c0:c0 + 1], ssq_[:, c0 + 1:c0 + 2], ALU.add, [("pt", p, "s0"), ("pt", p, "s1")], [("pt", p, "rstd")])
        rsqrt(rstd_[:, c0:c0 + 1], rstd_[:, c0:c0 + 1], 1.0 / D, 0, [("pt", p, "rstd")], [("pt", p, "rstd")], "pt")
        stt(otmp_[:, 0:512], pA[:, :], rstd_[:, c0:c0 + 1], growb_[:, 0:512], ALU.mult, ALU.mult, [kA, ("pt", p, "rstd"), "growb"], [("otmp", p)])
        stt(otmp_[:, 512:1024], pB[:, :], rstd_[:, c0:c0 + 1], growb_[:, 512:1024], ALU.mult, ALU.mult, [kB, ("pt", p, "rstd"), "growb"], [("otmp", p)])
        tt("dve", xt[:, :], xt[:, :], otmp_[:, :], ALU.add, [kx, ("otmp", p)], [kx])
        dma("sp", tiles(dst)[:, n, :], xt[:, :], [kx], [("dst", n)])

    def ffn_phase(l, src, dst, gname, gout_idx, final):
        with ExitStack() as sf:
            def sbf(name, shape, dt=F32):
                return sf.enter_context(nc.sbuf_tensor("sf%d_%s" % (l, name), list(shape), dt))
            hT = sbf("hT", [128, 8, NTOK], BF16)
            Wd = sbf("Wd", [128, NF, D], BF16)
            actb = sbf("act", [128, NF, 1024], BF16)
            wg = [sbf("wg%d" % i, [128, 8, 256], BF16) for i in range(3)]
            wu = [sbf("wu%d" % i, [128, 8, 256], BF16) for i in range(3)]
            xt4 = [sbf("xt4%d" % i, [128, 2, D]) for i in range(2)]
            xs4 = [sbf("xs4%d" % i, [128, 2, D], BF16) for i in range(2)]
            sil = [sbf("sil%d" % i, [128, 512]) for i in range(2)]
            growb_ = sbf("growb", [128, D])
            otmp_ = [sbf("otmp%d" % i, [128, D]) for i in range(2)]
            ssq_ = sbf("ssq", [128, 8])
            rstd_ = sbf("rstd", [128, 8])
            dma("sp", growb_[:, :], brow_d[gout_idx], [], ["growb"])
            for i, (f0, f1) in enumerate([(0, 6), (6, 12), (12, 17), (17, 22)]):
                dma("pool", Wd[:, f0:f1, :], ffd_d[l].rearrange("(f p) e -> p f e", p=128)[:, f0:f1, :], [], [("Wd", i)])
            Wdk = [("Wd", i) for i in range(4)]
            norm_to_fm(None, src, hT, gname, xt4, xs4, ssq_, rstd_)
            hk = lambda tb: [("hT", dc, tb) for dc in range(8)]
            it = 0
            for half in range(2):
                for f in range(NF):
                    bi_ = (f // 2) % 3
                    fo = (f % 2) * 128
                    if f % 2 == 0:
                        dma("pool", wg[bi_][:, :, :], ffg_d[l].rearrange("(k p) f -> p k f", p=128)[:, :, f * 128:(f + 2) * 128], [], [("wg", bi_)])
                        dma("pool", wu[bi_][:, :, :], ffu_d[l].rearrange("(k p) f -> p k f", p=128)[:, :, f * 128:(f + 2) * 128], [], [("wu", bi_)])
                    for tb in range(2):
                        gtb = half * 2 + tb
                        pg, pu = ps[(it % 2) * 2], ps[(it % 2) * 2 + 1]
                        kg, ku = ("ps", (it % 2) * 2), ("ps", (it % 2) * 2 + 1)
                        sl_ = sil[it % 2]
                        it += 1
                        for dc in range(8):
                            mm(pg[:, :], wg[bi_][:, dc, fo:fo + 128], hT[:, dc, gtb * 512:(gtb + 1) * 512], dc == 0, dc == 7,
                               [("wg", bi_), ("hT", dc, gtb)], [kg])
                        for dc in range(8):
                            mm(pu[:, :], wu[bi_][:, dc, fo:fo + 128], hT[:, dc, gtb * 512:(gtb + 1) * 512], dc == 0, dc == 7,
                               [("wu", bi_), ("hT", dc, gtb)], [ku])
                        act(sl_[:, :], pg[:, :], AF.Silu, [kg], [("sil", it % 2)])
                        tt("dve", actb[:, f, tb * 512:(tb + 1) * 512], pu[:, :], sl_[:, :], ALU.mult, [ku, ("sil", it % 2)], [("act", f, tb)])
                for tl in range(8):
                    n = half * 8 + tl
                    pA, pB = ps[4 + (tl % 2) * 2], ps[5 + (tl % 2) * 2]
                    kA, kB = ("ps", 4 + (tl % 2) * 2), ("ps", 5 + (tl % 2) * 2)
                    for f in range(NF):
                        mm(pA[:, :], actb[:, f, tl * 128:(tl + 1) * 128], Wd[:, f, 0:512], f == 0, f == NF - 1,
                           [("act", f, tl // 4)] + Wdk, [kA])
                    for f in range(NF):
                        mm(pB[:, :], actb[:, f, tl * 128:(tl + 1) * 128], Wd[:, f, 512:1024], f == 0, f == NF - 1,
                           [("act", f, tl // 4)] + Wdk, [kB])
                    post_tail(pA, pB, kA, kB, src, dst, n, xt4[tl % 2][:, 0, :], otmp_[tl % 2], growb_, ssq_, rstd_, p=tl % 2)
            sched.flush(rec, final=final)

    def sgu_phase(src, dst):
        with ExitStack() as sf:
            def sbf(name, shape, dt=F32):
                return sf.enter_context(nc.sbuf_tensor("sg_%s" % name, list(shape), dt))
            hT = sbf("hT", [128, 8, NTOK], BF16)
            Win = sbf("Win", [128, 8, 2 * D], BF16)
            Wout = sbf("Wout", [128, 8, D], BF16)
            uT = sbf("uT", [128, 8, NTOK], BF16)
            ws_sb = sbf("ws_sb", [128, 8, 128], BF16)
            wsT = sbf("wsT", [128, 8, 128], BF16)
            Cg = sbf("Cg", [128, 8, 128])
            onesb = sbf("onesb", [128, 128], BF16)
            rowb = sbf("rowb", [1, 2048 + 128], BF16)
            xt4 = [sbf("xt4%d" % i, [128, 2, D]) for i in range(2)]
            xs4 = [sbf("xs4%d" % i, [128, 2, D], BF16) for i in range(2)]
            vgp = [sbf("vg%d" % i, [128, D]) for i in range(2)]
            zbp = [sbf("zb%d" % i, [128, D], BF16) for i in range(2)]
            tmpc = sbf("tmpc", [128, 512])
            tmpcp = [sbf("tmpct%d" % i, [128, 512]) for i in range(2)]
            growb_ = sbf("growb", [128, D])
            otmp_ = [sbf("otmp%d" % i, [128, D]) for i in range(2)]
            ssq_ = sbf("ssq", [128, 8])
            rstd_ = sbf("rstd", [128, 8])
            st4p = [sbf("st4%d" % i, [128, 8]) for i in range(2)]
            dma("sp", growb_[:, :], brow_d[2], [], ["growb"])
            for q4 in range(4):
                dma("pool", Win[:, q4 * 2:(q4 + 1) * 2, :], sgu_in_d.rearrange("(k p) e -> p k e", p=128)[:, q4 * 2:(q4 + 1) * 2, :], [], [("Win", q4)])
            Wink = [("Win", q4) for q4 in range(4)]
            for hf in range(2):
                dma("pool", Wout[:, hf * 4:(hf + 1) * 4, :], sgu_out_d.rearrange("(k p) e -> p k e", p=128)[:, hf * 4:(hf + 1) * 4, :], [], [("Wout", hf)])
            Woutk = [("Wout", 0), ("Wout", 1)]
            dma("pool", ws_sb[:, :, :], sgu_ws_d.rearrange("g i j -> i g j"), [], ["ws_sb"])
            dma("pool", rowb[:, :], rowv_d[:, :], [], ["rowb"])
            rec.add("pool", lambda e: e.memset(onesb[:, :], 1.0), [], ["onesb"])
            for g in range(8):
                pb = ps[6]
                mm(pb[:, 0:128], ws_sb[:, g, :], identb[:, :], True, True, ["ws_sb", "identb"], [("ps", 6)])
                cp("dve", wsT[:, g, :], pb[:, 0:128], [("ps", 6)], [("wsT", g)])
                rec.add("pool", lambda e, g=g: e.memset(wsT[64:128, g, 0:64], 0.0), [("wsT", g)], [("wsT", g)])
                mm(pb[:, 128:256], onesb[:, :], wsT[:, g, :], True, True, ["onesb", ("wsT", g)], [("ps", 6)])
                mm(pb[:, 256:384], onesb[0:1, :], rowb[0:1, 1024 + g * 128:1024 + (g + 1) * 128], True, True, ["onesb", "rowb"], [("ps", 6)])
                cp("dve", tmpc[:, 0:128], pb[:, 256:384], [("ps", 6)], ["tmpc"])
                stt(Cg[:, g, :], pb[:, 128:256], pc("lnb", g), tmpc[:, 0:128], ALU.mult, ALU.add, [("ps", 6), "pcol", "tmpc"], [("Cg", g)])
            norm_to_fm(None, src, hT, "g10", xt4, xs4, ssq_, rstd_)
            it = 0
            for g in range(8):
                for tb in range(4):
                    pb = ps[it % 2]
                    kb = ("ps", it % 2)
                    it += 1
                    for dc in range(8):
                        mm(pb[:, :], Win[:, dc, g * 128:(g + 1) * 128], hT[:, dc, tb * 512:(tb + 1) * 512], dc == 0, dc == 7,
                           Wink + [("hT", dc, tb)], [kb])
                    act(uT[:, g, tb * 512:(tb + 1) * 512], pb[:, :], AF.Gelu, [kb, "pcol"], [("uT", g, tb * 4 + i_) for i_ in range(4)], bias=pc("binu", g))
            def sgu_tile(n):
                tb = n // 4
                p = n % 2
                vg, zb, tmpct, st4 = vgp[p], zbp[p], tmpcp[p], st4p[p]
                kv = [("ps", 4 * p), ("ps", 4 * p + 1)]
                pv = [ps[4 * p], ps[4 * p + 1]]
                for eh in range(2):
                    for dc in range(8):
                        mm(pv[eh][:, :], hT[:, dc, n * 128:(n + 1) * 128], Win[:, dc, D + eh * 512:D + (eh + 1) * 512], dc == 0, False,
                           Wink + [("hT", dc, tb)], [kv[eh]])
                    mm(pv[eh][:, :], onesb[0:1, :], rowb[0:1, eh * 512:(eh + 1) * 512], False, True, ["onesb", "rowb"], [kv[eh]])
                    act(vg[:, eh * 512:(eh + 1) * 512], pv[eh][:, :], AF.Gelu, [kv[eh]], [("vg", p, eh)], accum=st4[:, eh:eh + 1])
                    yield
                act(junk[:, :], vg[:, :], AF.Square, [("vg", p, 0), ("vg", p, 1)], ["junk", ("st4", p, 2)], accum=st4[:, 2:3])
                tt("dve", st4[:, 3:4], st4[:, 0:1], st4[:, 1:2], ALU.add, [("vg", p, 0), ("vg", p, 1)], [("st4", p, 3)])
                ts("dve", st4[:, 3:4], st4[:, 3:4], 1.0 / D, None, ALU.mult, None, [("st4", p, 3)], [("st4", p, 3)])
                tt("dve", st4[:, 5:6], st4[:, 3:4], st4[:, 3:4], ALU.mult, [("st4", p, 3)], [("st4", p, 5)])
                stt(st4[:, 4:5], st4[:, 2:3], 1.0 / D, st4[:, 5:6], ALU.mult, ALU.subtract, [("st4", p, 2), ("st4", p, 5)], [("st4", p, 4)])
                rsqrt(st4[:, 4:5], st4[:, 4:5], 1.0, 3, [("st4", p, 4)], [("st4", p, 4)], "ln")
                yield
                ts("dve", zb[:, :], vg[:, :], st4[:, 3:4], st4[:, 4:5], ALU.subtract, ALU.mult,
                   [("vg", p, 0), ("vg", p, 1), ("st4", p, 3), ("st4", p, 4)], [("zb", p)])
                yield
                for g in range(8):
                    pz = ps[4 * p + 2 + g // 4]
                    kz = ("ps", 4 * p + 2 + g // 4)
                    mm(pz[:, (g % 4) * 128:(g % 4 + 1) * 128], zb[:, g * 128:(g + 1) * 128], wsT[:, g, :], True, True, [("zb", p), ("wsT", g)], [kz])
                yield
                for g in range(8):
                    pz = ps[4 * p + 2 + g // 4]
                    kz = ("ps", 4 * p + 2 + g // 4)
                    stt(tmpct[:, (g % 4) * 128:(g % 4 + 1) * 128], pz[:, (g % 4) * 128:(g % 4 + 1) * 128], pc("lnw", g), Cg[:, g, :], ALU.mult, ALU.add,
                        [kz, "pcol", ("Cg", g)], [("tmpct", p)])
                    tt("dve", uT[:, g, n * 128:(n + 1) * 128], uT[:, g, n * 128:(n + 1) * 128], tmpct[:, (g % 4) * 128:(g % 4 + 1) * 128], ALU.mult,
                       [("uT", g, n), ("tmpct", p)], [("uT", g, n)])
                    if g % 4 == 3:
                        yield
                pA, pB = pv
                for cc in range(8):
                    mm(pA[:, :], uT[:, cc, n * 128:(n + 1) * 128], Wout[:, cc, 0:512], cc == 0, cc == 7, [("uT", cc, n)] + Woutk, [kv[0]])
                for cc in range(8):
                    mm(pB[:, :], uT[:, cc, n * 128:(n + 1) * 128], Wout[:, cc, 512:1024], cc == 0, cc == 7, [("uT", cc, n)] + Woutk, [kv[1]])
                yield
                post_tail(pA, pB, kv[0], kv[1], src, dst, n, xt4[p][:, 0, :], otmp_[p], growb_, ssq_, rstd_, p=p)
                yield

            pend = [sgu_tile(n) for n in range(16)]
            active = []
            while active or pend:
                if pend and len(active) < 2:
                    active.append(pend.pop(0))
                for g_ in list(active):
                    try:
                        next(g_)
                    except StopIteration:
                        active.remove(g_)
            sched.flush(rec, final=False)

    ffn_phase(0, x1s, out_d if stage == 3 else x2s, "g02", 1, stage == 3)
    if stage == 3:
        return nc, st
    sgu_phase(x2s, out_d if stage == 4 else x1s)
    if stage == 4:
        sched.flush(rec, final=True)
        return nc, st
    ffn_phase(1, x1s, out_d, "g12", 3, True)
    return nc, st


def _col(v):
    return np.ascontiguousarray(np.asarray(v, np.float32).reshape(8, 128).T)


def _consts():
    p = np.arange(128)[:, None]
    f = np.arange(128)[None, :]
    ident = (p == f).astype(np.float32)
    mSL = np.tile((p < f).astype(np.float32), (1, 4))
    mLE = np.tile((p <= f).astype(np.float32), (1, 4))
    mGT = np.tile((p > f).astype(np.float32), (1, 4))
    keep = np.ones((128, 256), np.float32)
    keep[:, 0::128] = 0.0
    bones = ((p // 64) == (f // 64)).astype(np.float32)
    sel = ident.copy()
    ifold = ((p % 64) == np.arange(64)[None, :]).astype(np.float32)
    return np.concatenate([ident, mSL, mLE, mGT, keep, bones, sel, ifold], axis=1).astype(np.float32)


_CACHE = {}


def kernel(**inp):
    inp = {k: np.asarray(v) for k, v in inp.items()}
    x = inp["x"].astype(np.float32)
    ng = inp["norm_gains"]
    pcol = np.zeros((128, NPC), np.float32)
    for c in range(6):
        pcol[:, PC["mix"] + c * 8:PC["mix"] + c * 8 + 8] = _col(inp["rwkv_mix"][0, c])
    for nm, v in [("g00", ng[0, 0]), ("w0", inp["rwkv_w0"][0]), ("a0", inp["rwkv_a0"][0]), ("kk", inp["rwkv_k_k"][0]),
                  ("ka", inp["rwkv_k_a"][0]), ("rk", inp["rwkv_r_k"][0].reshape(-1)), ("gnw", inp["rwkv_gn_w"][0]),
                  ("gnb", inp["rwkv_gn_b"][0]), ("g02", ng[0, 2]), ("g10", ng[1, 0]), ("g12", ng[1, 2]),
                  ("binu", inp["sgu_b_in"][0, :1024]), ("lnw", inp["sgu_ln_w"][0]), ("lnb", inp["sgu_ln_b"][0])]:
        pcol[:, PC[nm]:PC[nm] + 8] = _col(v)
    pcol[:64, PC["hm"] + 0] = 1.0
    pcol[64:, PC["hm"] + 1] = 1.0
    pcol[:64, PC["hm"] + 2] = -1.0
    pcol[64:, PC["hm"] + 3] = -1.0
    brow = np.stack([np.broadcast_to(ng[i, j][None, :], (128, D)) for (i, j) in [(0, 1), (0, 3), (1, 1), (1, 3)]]).astype(np.float32)
    rowv = np.concatenate([inp["sgu_b_in"][0, 1024:], inp["sgu_bs"][0].reshape(-1), np.ones(128, np.float32)])[None, :].astype(np.float32)
    common = dict(
        pcol=pcol, brow=np.ascontiguousarray(brow), cst=_consts(), rowv=rowv,
        w_rkv=inp["rwkv_w_rkv"][0], w_o=inp["rwkv_w_o"][0], w1=inp["rwkv_w1"][0], w2=inp["rwkv_w2"][0],
        a1=inp["rwkv_a1"][0], a2=inp["rwkv_a2"][0], g1=inp["rwkv_g1"][0], g2=inp["rwkv_g2"][0],
        sgu_in=inp["sgu_w_in"][0], sgu_ws=inp["sgu_ws"][0], sgu_out=inp["sgu_w_out"][0],
        ffg=inp["ffn_w_gate"], ffu=inp["ffn_w_up"], ffd=inp["ffn_w_down"],
    )
    common = {k: np.ascontiguousarray(v, dtype=np.float32) for k, v in common.items()}
    in_maps = []
    for c in range(8):
        b, hf = c // 2, c % 2
        m = dict(common)
        m["xo"] = np.ascontiguousarray(x[b, hf * NTOK:(hf + 1) * NTOK])
        m["xp"] = np.ascontiguousarray(x[b, 0:NTOK]) if hf == 1 else np.zeros((NTOK, D), np.float32)
        in_maps.append(m)
    if "nc" not in _CACHE:
        _CACHE["nc"] = build_program()
    nc, _st = _CACHE["nc"]
    res = run_bass_kernel_spmd(nc, in_maps, core_ids=list(range(8)))
    out = np.zeros((4, 2 * NTOK, D), np.float32)
    for c in range(8):
        b, hf = c // 2, c % 2
        out[b, hf * NTOK:(hf + 1) * NTOK] = res.results[c]["out"]
    return out
```
